# Optimizing a Trainium2 kernel written in Bass

```python
import jax, jax.numpy as jnp
from jax import lax
import numpy as np

D_MODEL = 1024
BATCH = 8
SEQ = 8192
DEPTH = 1

N_MEM = 256
BLOCK = 128
EPS = 1e-6
NEG = -1e30
ROPE_THETA = 500000.0

DIL_GROUPS = ((128, 1), (512, 4), (2048, 16))
N_DIL_GROUPS = 3
A_HEADS = 4
A_HEAD_DIM = 128
A_WIDTH = A_HEADS * A_HEAD_DIM
A_QKV = N_DIL_GROUPS * A_WIDTH
ROT_DIM = A_HEAD_DIM // 4

B_HEADS = 8
B_HEAD_DIM = 64
B_WIDTH = B_HEADS * B_HEAD_DIM

M_HEADS = 4
M_HEAD_DIM = 128
M_WIDTH = M_HEADS * M_HEAD_DIM

N_BRANCH = 3
IN_SIZES = (A_QKV, A_QKV, A_QKV, A_WIDTH,
            B_WIDTH, B_WIDTH, B_WIDTH, B_HEADS, B_WIDTH,
            M_WIDTH, M_WIDTH,
            N_BRANCH * D_MODEL)
IN_COLS = 3 * A_QKV + A_WIDTH + 4 * B_WIDTH + B_HEADS + 2 * M_WIDTH + N_BRANCH * D_MODEL

kernel_name = "hybrid_dilated_fox_memory_gated_block"


def rmsnorm(x, g):
    x32 = x.astype(jnp.float32)
    y = x32 * lax.rsqrt(jnp.mean(x32 * x32, axis=-1, keepdims=True) + EPS)
    return (y * g.astype(jnp.float32)).astype(x.dtype)


def rope_partial(x, pos):
    half = ROT_DIM // 2
    inv = ROPE_THETA ** (-jnp.arange(half, dtype=jnp.float32) / half)
    ang = pos.astype(jnp.float32)[..., None] * inv
    cos = jnp.cos(ang)[:, :, None, :]
    sin = jnp.sin(ang)[:, :, None, :]
    xr = x[..., :ROT_DIM].astype(jnp.float32)
    x1, x2 = xr[..., :half], xr[..., half:]
    rot = jnp.concatenate([x1 * cos - x2 * sin, x2 * cos + x1 * sin], axis=-1).astype(x.dtype)
    return jnp.concatenate([rot, x[..., ROT_DIM:]], axis=-1)


def dilated_window_attention(q, k, v, window, dilation):
    bsz, seq, nh, hd = q.shape
    L = seq // dilation
    steps = window // dilation
    Lp = -(-L // BLOCK) * BLOCK
    nb = Lp // BLOCK

    def to_classes(t):
        t = t.reshape(bsz, L, dilation, nh, hd).transpose(0, 2, 3, 1, 4)
        return jnp.pad(t, ((0, 0), (0, 0), (0, 0), (0, Lp - L), (0, 0)))

    def band(t):
        t = jnp.pad(t, ((0, 0), (0, 0), (0, 0), (BLOCK, 0), (0, 0)))
        t = t.reshape(bsz, dilation, nh, nb + 1, BLOCK, hd)
        return jnp.concatenate([t[:, :, :, :-1], t[:, :, :, 1:]], axis=4)

    qb = to_classes(q).reshape(bsz, dilation, nh, nb, BLOCK, hd)
    kb = band(to_classes(k))
    vb = band(to_classes(v))
    s = jnp.einsum('brhnqd,brhnkd->brhnqk', qb, kb).astype(jnp.float32) * (hd ** -0.5)
    qi = jnp.arange(BLOCK)[:, None]
    ki = jnp.arange(2 * BLOCK)[None, :]
    rel = BLOCK + qi - ki
    blk = jnp.arange(nb)[:, None, None]
    valid = (rel >= 0) & (rel <= steps) & (blk * BLOCK + ki - BLOCK >= 0)
    s = jnp.where(valid, s, NEG)
    m = jnp.max(s, axis=-1, keepdims=True)
    p = jnp.exp(s - m)
    den = jnp.sum(p, axis=-1)
    o = jnp.einsum('brhnqk,brhnkd->brhnqd', p.astype(v.dtype), vb).astype(jnp.float32) / den[..., None]
    lse = m[..., 0] + jnp.log(den)
    o = o.reshape(bsz, dilation, nh, Lp, hd)[:, :, :, :L].transpose(0, 3, 1, 2, 4).reshape(bsz, seq, nh, hd)
    lse = lse.reshape(bsz, dilation, nh, Lp)[..., :L].transpose(0, 3, 1, 2).reshape(bsz, seq, nh)
    return o, lse


def forgetting_attention(q, k, v, log_f):
    bsz, seq, nh, hd = q.shape
    nb = seq // BLOCK
    c = jnp.cumsum(log_f, axis=1).transpose(0, 2, 1)
    kt = k.transpose(0, 2, 1, 3)
    vt = v.transpose(0, 2, 1, 3)
    qb = q.transpose(0, 2, 1, 3).reshape(bsz, nh, nb, BLOCK, hd).transpose(2, 0, 1, 3, 4)
    cb = c.reshape(bsz, nh, nb, BLOCK).transpose(2, 0, 1, 3)
    kpos = jnp.arange(seq)

    def one_block(args):
        n, qn, cn = args
        s = jnp.einsum('bhqd,bhkd->bhqk', qn, kt).astype(jnp.float32) * (hd ** -0.5)
        s = s + cn[..., :, None] - c[:, :, None, :]
        qpos = n * BLOCK + jnp.arange(BLOCK)
        s = jnp.where(kpos[None, :] <= qpos[:, None], s, NEG)
        p = jax.nn.softmax(s, axis=-1)
        return jnp.einsum('bhqk,bhkd->bhqd', p.astype(vt.dtype), vt)

    o = lax.map(one_block, (jnp.arange(nb), qb, cb))
    return o.transpose(1, 0, 3, 2, 4).reshape(bsz, seq, nh, hd)


def memory_attention(q, mem_k, mem_v):
    s = jnp.einsum('bshd,bnhd->bhsn', q, mem_k).astype(jnp.float32) * (q.shape[-1] ** -0.5)
    p = jax.nn.softmax(s, axis=-1)
    return jnp.einsum('bhsn,bnhd->bshd', p.astype(mem_v.dtype), mem_v)


def setup_inputs(seed: int = 0) -> dict:
    key = jax.random.key(seed)
    ks = jax.random.split(key, 16)
    f32 = jnp.float32
    x = jax.random.normal(ks[0], (BATCH, SEQ, D_MODEL), f32)
    mem = jax.random.normal(ks[1], (BATCH, N_MEM, D_MODEL), f32)
    start = jax.random.randint(ks[2], (BATCH, 1), 0, 4096, dtype=jnp.int32)
    positions = start + jnp.arange(SEQ, dtype=jnp.int32)[None, :]
    norm_pre_g = 1.0 + 0.05 * jax.random.normal(ks[3], (DEPTH, D_MODEL), f32)
    norm_post_g = 1.0 + 0.05 * jax.random.normal(ks[4], (DEPTH, D_MODEL), f32)
    norm_mem_g = 1.0 + 0.05 * jax.random.normal(ks[5], (DEPTH, D_MODEL), f32)
    w_in = jax.random.normal(ks[6], (DEPTH, D_MODEL, IN_COLS), f32) * D_MODEL ** -0.5
    b_forget = 3.0 + 0.5 * jax.random.normal(ks[7], (DEPTH, B_HEADS), f32)
    b_merge = 0.01 * jax.random.normal(ks[8], (DEPTH, N_BRANCH * D_MODEL), f32)
    w_mem_kv = jax.random.normal(ks[9], (DEPTH, D_MODEL, 2 * M_WIDTH), f32) * D_MODEL ** -0.5
    w_branch_a = jax.random.normal(ks[10], (DEPTH, A_WIDTH, D_MODEL), f32) * A_WIDTH ** -0.5
    w_branch_b = jax.random.normal(ks[11], (DEPTH, B_WIDTH, D_MODEL), f32) * B_WIDTH ** -0.5
    w_branch_m = jax.random.normal(ks[12], (DEPTH, M_WIDTH, D_MODEL), f32) * M_WIDTH ** -0.5
    w_out = jax.random.normal(ks[13], (DEPTH, D_MODEL, D_MODEL), f32) * D_MODEL ** -0.5
    return {"x": x, "mem": mem, "positions": positions,
            "norm_pre_g": norm_pre_g, "norm_post_g": norm_post_g, "norm_mem_g": norm_mem_g,
            "w_in": w_in, "b_forget": b_forget, "b_merge": b_merge, "w_mem_kv": w_mem_kv,
            "w_branch_a": w_branch_a, "w_branch_b": w_branch_b, "w_branch_m": w_branch_m,
            "w_out": w_out}


def reference(x, mem, positions, norm_pre_g, norm_post_g, norm_mem_g, w_in, b_forget, b_merge,
              w_mem_kv, w_branch_a, w_branch_b, w_branch_m, w_out):
    bsz, seq, _ = x.shape
    split_at = [int(i) for i in np.cumsum(IN_SIZES)[:-1]]
    for layer in range(DEPTH):
        h = rmsnorm(x, norm_pre_g[layer])
        u = jnp.einsum('bsd,de->bse', h, w_in[layer])
        (qa, ka, va, za, qb, kb, vb, fb, zb, qm, zm, gl) = jnp.split(u, split_at, axis=-1)

        nh_a = N_DIL_GROUPS * A_HEADS
        qa = rope_partial(qa.reshape(bsz, seq, nh_a, A_HEAD_DIM), positions)
        ka = rope_partial(ka.reshape(bsz, seq, nh_a, A_HEAD_DIM), positions)
        va = va.reshape(bsz, seq, nh_a, A_HEAD_DIM)
        outs, lses = [], []
        for g, (window, dilation) in enumerate(DIL_GROUPS):
            sl = slice(g * A_HEADS, (g + 1) * A_HEADS)
            o_g, lse_g = dilated_window_attention(qa[:, :, sl], ka[:, :, sl], va[:, :, sl], window, dilation)
            outs.append(o_g)
            lses.append(lse_g)
        wgt = jax.nn.softmax(jnp.stack(lses, axis=0), axis=0)
        y_a = jnp.sum(wgt[..., None] * jnp.stack(outs, axis=0), axis=0)
        y_a = y_a.astype(x.dtype).reshape(bsz, seq, A_WIDTH) * jax.nn.silu(za)

        log_f = jax.nn.log_sigmoid((fb + b_forget[layer]).astype(jnp.float32))
        y_b = forgetting_attention(qb.reshape(bsz, seq, B_HEADS, B_HEAD_DIM),
                                   kb.reshape(bsz, seq, B_HEADS, B_HEAD_DIM),
                                   vb.reshape(bsz, seq, B_HEADS, B_HEAD_DIM), log_f)
        y_b = y_b.reshape(bsz, seq, B_WIDTH) * jax.nn.silu(zb)

        mkv = jnp.einsum('bnd,de->bne', rmsnorm(mem, norm_mem_g[layer]), w_mem_kv[layer])
        mk, mv = jnp.split(mkv, 2, axis=-1)
        y_m = memory_attention(qm.reshape(bsz, seq, M_HEADS, M_HEAD_DIM),
                               mk.reshape(bsz, N_MEM, M_HEADS, M_HEAD_DIM),
                               mv.reshape(bsz, N_MEM, M_HEADS, M_HEAD_DIM))
        y_m = y_m.reshape(bsz, seq, M_WIDTH) * jax.nn.silu(zm)

        gates = jax.nn.sigmoid(gl + b_merge[layer]).reshape(bsz, seq, N_BRANCH, D_MODEL)
        merged = (gates[:, :, 0] * jnp.einsum('bse,ed->bsd', y_a, w_branch_a[layer])
                  + gates[:, :, 1] * jnp.einsum('bse,ed->bsd', y_b, w_branch_b[layer])
                  + gates[:, :, 2] * jnp.einsum('bse,ed->bsd', y_m, w_branch_m[layer]))
        out = jnp.einsum('bsd,de->bse', merged, w_out[layer])
        x = x + rmsnorm(out, norm_post_g[layer])
    return x
```

```python
import numpy as np
from contextlib import ExitStack
import concourse.bass as bass
import concourse.mybir as mybir
from concourse.bass_utils import run_bass_kernel_spmd

F32 = mybir.dt.float32
BF16 = mybir.dt.bfloat16
I32 = mybir.dt.int32
AF = mybir.ActivationFunctionType
ALU = mybir.AluOpType

D = 1024
KC = 8
NMEM = 256
IN_COLS = 11272
QA, KA, VA, ZA = 0, 1536, 3072, 4608
QB, KB, VB, FBO, ZB = 5120, 5632, 6144, 6656, 6664
QM, ZM, GL = 7176, 7688, 8200
EPS = 1e-6
NEGM = -30000.0
TWO_PI = float(2.0 * np.pi)
PI = float(np.pi)
DILS = (1, 4, 16)


class DSem:
    def __init__(self, sem):
        self.sem = sem
        self.n = 0


class Bld:
    def __init__(self, nc, es):
        self.nc = nc
        self.es = es
        self.eng = {"pe": nc.tensor, "act": nc.scalar, "dve": nc.vector, "pool": nc.gpsimd, "sp": nc.sync}
        self.sem = {e: es.enter_context(nc.semaphore("s_" + e)) for e in ("pe", "act", "dve", "pool")}
        self.cnt = {e: 0 for e in self.sem}
        self.seen = {e: {} for e in self.eng}
        self.dsems = []
        self.nds = 0

    def sig(self, e, inst):
        inst.then_inc(self.sem[e], 1)
        self.cnt[e] += 1
        return (self.sem[e], self.cnt[e])

    def last(self, e):
        return (self.sem[e], self.cnt[e])

    def wait(self, e, *toks):
        for t in toks:
            if t is None:
                continue
            if isinstance(t, list):
                self.wait(e, *t)
                continue
            sem, v = t
            if v <= 0:
                continue
            k = id(sem)
            if self.seen[e].get(k, 0) >= v:
                continue
            self.eng[e].wait_ge(sem, v)
            self.seen[e][k] = v

    def dsem(self, track=True):
        s = self.es.enter_context(self.nc.semaphore("d%d" % self.nds))
        self.nds += 1
        d = DSem(s)
        if track:
            self.dsems.append(d)
        return d

    def dma(self, q, out, in_, ds):
        inst = self.eng[q].dma_start(out=out, in_=in_)
        ds.n += 16
        inst.then_inc(ds.sem, 16)
        return (ds.sem, ds.n)

    def barrier(self):
        toks = [self.last(e) for e in self.sem] + [(d.sem, d.n) for d in self.dsems]
        for e in self.eng:
            self.wait(e, *toks)


def build_program(S):
    NB = S // 512
    NT = S // 128
    nc = bass.Bass("TRN2", target_bir_lowering=False)

    def din(name, shape, dt=F32):
        return nc.dram_tensor(name, shape, dt, kind="ExternalInput").ap()

    def dscr(name, shape, dt):
        return nc.dram_tensor(name, shape, dt, kind="Internal").ap()

    xT_d = din("xT", [D, S])
    x_d = din("x", [S, D])
    memT_d = din("memT", [D, NMEM])
    posr_d = din("posr", [64, S], I32)
    w_in_d = din("w_in", [D, IN_COLS])
    w_mem_d = din("w_mem", [D, 1024])
    w_a_d = din("w_a", [512, D])
    w_b_d = din("w_b", [512, D])
    w_m_d = din("w_m", [512, D])
    w_out_d = din("w_out", [D, D])
    gpre_d = din("gpre", [128, KC])
    gmem_d = din("gmem", [128, KC])
    gpost_d = din("gpost", [128, D])
    bmerge_d = din("bmerge", [128, 24])
    bfg_d = din("bfg", [8, 1])
    c_ident_d = din("c_ident", [128, 128])
    c_mcur_d = din("c_mcur", [128, 128])
    c_mprev_d = din("c_mprev", [128, 128])
    c_perm_d = din("c_perm", [32, 32])
    c_inv_d = din("c_inv", [64, 2])
    y_d = nc.dram_tensor("y", [S, D], F32, kind="ExternalOutput").ap()

    wbf_d = dscr("wbf", [D, IN_COLS], BF16)
    wmembf_d = dscr("wmembf", [D, 1024], BF16)
    wabf_d = dscr("wabf", [512, D], BF16)
    wbbf_d = dscr("wbbf", [512, D], BF16)
    wmbf_d = dscr("wmbf", [512, D], BF16)
    woutbf_d = dscr("woutbf", [D, D], BF16)
    hT_d = dscr("hT", [D, S], BF16)
    cpos_d = dscr("cpos", [8, 3, S], BF16)
    cneg_d = dscr("cneg", [8, 3, S], BF16)
    yga_d = dscr("yga", [512, S], BF16)
    ygb_d = dscr("ygb", [512, S], BF16)
    ygm_d = dscr("ygm", [512, S], BF16)

    hT_v = hT_d.rearrange("(k p) s -> p k s", p=128)
    xT_v = xT_d.rearrange("(k p) s -> p k s", p=128)
    wbf_v = wbf_d.rearrange("(k p) c -> p k c", p=128)

    with ExitStack() as es:
        B = Bld(nc, es)
        pe, act, dve, pool, sp = nc.tensor, nc.scalar, nc.vector, nc.gpsimd, nc.sync

        def sb(stack, name, shape, dt):
            return stack.enter_context(nc.sbuf_tensor("sb_" + name, shape, dt))

        psA = [es.enter_context(nc.psum_tensor("psA%d" % i, [128, 1024], F32)) for i in range(4)]
        ps = []
        for i in range(4):
            ps.append(psA[i][:, 0:512])
            ps.append(psA[i][:, 512:1024])

        wsem = {}

        def cast_region(name, dst, src, rows, c0, c1):
            ds = wsem.get(name)
            if ds is None:
                ds = wsem[name] = B.dsem(track=False)
            c = c0
            while c < c1:
                ce = min(c + 2048, c1)
                for r0 in range(0, rows, 128):
                    B.dma("pool", dst[r0:r0 + 128, c:ce], src[r0:r0 + 128, c:ce], ds)
                c = ce

        def wtok(name):
            return (wsem[name].sem, wsem[name].n)

        ones_bf = sb(es, "ones_bf", [128, 128], BF16)
        ident_bf = sb(es, "ident_bf", [128, 128], BF16)
        mcur_bf = sb(es, "mcur_bf", [128, 128], BF16)
        mprev_bf = sb(es, "mprev_bf", [128, 128], BF16)
        perm_bf = sb(es, "perm_bf", [32, 32], BF16)
        gpre = sb(es, "gpre", [128, KC], F32)
        gmem = sb(es, "gmem", [128, KC], F32)
        gpost = sb(es, "gpost", [128, D], F32)
        bmerge = sb(es, "bmerge", [128, 24], F32)
        nbf = sb(es, "nbf", [8, 1], F32)
        cinv = sb(es, "cinv", [64, 2], F32)
        KmT = sb(es, "KmT", [128, 4, NMEM], BF16)
        Vm = sb(es, "Vm", [128, 2, 512], BF16)

        cds = B.dsem()
        cdp = B.dsem()
        B.dma("pool", ident_bf[:], c_ident_d[:, :], cdp)
        B.dma("pool", mcur_bf[:], c_mcur_d[:, :], cdp)
        B.dma("pool", mprev_bf[:], c_mprev_d[:, :], cdp)
        ctokp = B.dma("pool", perm_bf[:], c_perm_d[:, :], cdp)
        B.dma("sp", gpre[:], gpre_d[:, :], cds)
        B.dma("sp", gmem[:], gmem_d[:, :], cds)
        B.dma("sp", gpost[:], gpost_d[:, :], cds)
        B.dma("sp", bmerge[:], bmerge_d[:, :], cds)
        B.dma("sp", nbf[:], bfg_d[:, :], cds)
        ctok = B.dma("sp", cinv[:], c_inv_d[:, :], cds)

        cast_region("wmem", wmembf_d, w_mem_d, D, 0, 1024)
        cast_region("wfox", wbf_d, w_in_d, D, QB, QM)
        cast_region("wA", wbf_d, w_in_d, D, 0, QB)
        cast_region("wM", wbf_d, w_in_d, D, QM, GL)
        cast_region("wG", wbf_d, w_in_d, D, GL, IN_COLS)
        cast_region("wa", wabf_d, w_a_d, 512, 0, D)
        cast_region("wb", wbbf_d, w_b_d, 512, 0, D)
        cast_region("wm", wmbf_d, w_m_d, 512, 0, D)
        cast_region("wout", woutbf_d, w_out_d, D, 0, D)

        B.wait("dve", ctok, ctokp)
        t0 = B.sig("dve", dve.memset(ones_bf[:], 1.0))
        B.wait("dve", t0)
        t_nbf = B.sig("dve", dve.tensor_scalar(out=nbf[:], in0=nbf[:], scalar1=-1.0, scalar2=None, op0=ALU.mult))
        for e in ("pe", "act", "pool"):
            B.wait(e, ctok, ctokp, t0, t_nbf)

        with ExitStack() as p0:
            memt = sb(p0, "memt", [128, KC, NMEM], F32)
            msq = sb(p0, "msq", [128, KC, NMEM], BF16)
            hm = sb(p0, "hm", [128, KC, NMEM], BF16)
            mrt = sb(p0, "mrt", [128, NMEM], F32)
            mri = sb(p0, "mri", [128, NMEM], F32)
            wmem_sb = sb(p0, "wmem_sb", [128, KC, 1024], BF16)
            lds = B.dsem()
            lds_w = B.dsem()
            tl = B.dma("sp", memt[:], memT_d.rearrange("(k p) n -> p k n", p=128), lds)
            B.wait("sp", wtok("wmem"))
            tw = B.dma("sp", wmem_sb[:], wmembf_d.rearrange("(k p) c -> p k c", p=128), lds_w)
            B.wait("pool", tl)
            tq = B.sig("pool", pool.tensor_tensor(out=msq[:], in0=memt[:], in1=memt[:], op=ALU.mult))
            B.wait("pe", tq)
            for k in range(KC):
                mm = pe.matmul(ps[0][:, 0:NMEM], lhsT=ones_bf[:], rhs=msq[:, k, :], start=(k == 0), stop=(k == KC - 1))
            tp = B.sig("pe", mm)
            B.wait("dve", tp)
            t1 = B.sig("dve", dve.tensor_scalar(out=mri[:], in0=ps[0][:, 0:NMEM], scalar1=1.0 / D, scalar2=EPS, op0=ALU.mult, op1=ALU.add))
            B.wait("act", t1)
            ta = B.sig("act", act.activation(out=mrt[:], in_=mri[:], func=AF.Sqrt))
            B.wait("dve", ta)
            t2 = B.sig("dve", dve.reciprocal(out=mri[:], in_=mrt[:]))
            B.wait("dve", t2)
            for k in range(KC):
                th = B.sig("dve", dve.scalar_tensor_tensor(out=hm[:, k, :], in0=memt[:, k, :], scalar=gmem[:, k:k + 1], in1=mri[:], op0=ALU.mult, op1=ALU.mult))
            B.wait("pe", th, tw)
            for hh in range(4):
                for k in range(KC):
                    mm = pe.matmul(ps[1 + hh][:, 0:NMEM], lhsT=wmem_sb[:, k, hh * 128:(hh + 1) * 128], rhs=hm[:, k, :], start=(k == 0), stop=(k == KC - 1))
                tk = B.sig("pe", mm)
                B.wait("act", tk)
                B.sig("act", act.copy(out=KmT[:, hh, :], in_=ps[1 + hh][:, 0:NMEM]))
            for kt in range(2):
                for k in range(KC):
                    mm = pe.matmul(ps[5 + kt][:, :], lhsT=hm[:, k, kt * 128:(kt + 1) * 128], rhs=wmem_sb[:, k, 512:1024], start=(k == 0), stop=(k == KC - 1))
                tk = B.sig("pe", mm)
                B.wait("dve", tk)
                B.sig("dve", dve.tensor_copy(out=Vm[:, kt, :], in_=ps[5 + kt][:, :]))
            B.barrier()

        with ExitStack() as p0:
            FBt = sb(p0, "FBt", [8, S], F32)
            with ExitStack() as p0b:
                xt = [sb(p0b, "xt%d" % i, [128, KC, 512], F32) for i in range(2)]
                sq = [sb(p0b, "sq%d" % i, [128, KC, 512], BF16) for i in range(2)]
                hb = [sb(p0b, "hb%d" % i, [128, KC, 512], BF16) for i in range(2)]
                rt = [sb(p0b, "rt%d" % i, [128, 512], F32) for i in range(2)]
                ri = [sb(p0b, "ri%d" % i, [128, 512], F32) for i in range(2)]
                wfb = sb(p0b, "wfb", [128, KC, 8], BF16)
                lds = [B.dsem() for _ in range(2)]
                sds = [B.dsem() for _ in range(2)]
                wds = B.dsem()
                B.wait("sp", wtok("wfox"))
                twf = B.dma("sp", wfb[:], wbf_v[:, :, FBO:FBO + 8], wds)
                xt_free = [None, None]
                sq_free = [None, None]
                ssp_free = [None, None]
                rt_free = [None, None]
                ri_free = [None, None]
                hb_free = [[], []]
                fbp_free = [None, None]
                ltok = {}

                def load_x(blk):
                    s_ = blk % 2
                    B.wait("sp", xt_free[s_])
                    ltok[blk] = B.dma("sp", xt[s_][:], xT_v[:, :, blk * 512:(blk + 1) * 512], lds[s_])
                load_x(0)
                for blk in range(NB):
                    s = blk % 2
                    c0 = blk * 512
                    if blk + 1 < NB:
                        load_x(blk + 1)
                    tl = ltok[blk]
                    B.wait("pool", tl, sq_free[s])
                    tq = B.sig("pool", pool.tensor_tensor(out=sq[s][:], in0=xt[s][:], in1=xt[s][:], op=ALU.mult))
                    B.wait("pe", tq, ssp_free[s])
                    for k in range(KC):
                        mm = pe.matmul(ps[s][:, :], lhsT=ones_bf[:], rhs=sq[s][:, k, :], start=(k == 0), stop=(k == KC - 1))
                    tp = B.sig("pe", mm)
                    sq_free[s] = tp
                    B.wait("dve", tp, rt_free[s])
                    t1 = B.sig("dve", dve.tensor_scalar(out=rt[s][:], in0=ps[s][:, :], scalar1=1.0 / D, scalar2=EPS, op0=ALU.mult, op1=ALU.add))
                    ssp_free[s] = t1
                    B.wait("act", t1)
                    ta = B.sig("act", act.activation(out=rt[s][:], in_=rt[s][:], func=AF.Sqrt))
                    B.wait("dve", ta, ri_free[s])
                    t2 = B.sig("dve", dve.reciprocal(out=ri[s][:], in_=rt[s][:]))
                    rt_free[s] = t2
                    B.wait("dve", t2, tl, *hb_free[s])
                    for k in range(KC):
                        th = B.sig("dve", dve.scalar_tensor_tensor(out=hb[s][:, k, :], in0=xt[s][:, k, :], scalar=gpre[:, k:k + 1], in1=ri[s][:], op0=ALU.mult, op1=ALU.mult))
                    xt_free[s] = [th, tq]
                    ri_free[s] = th
                    B.wait("sp", th)
                    tst = B.dma("sp", hT_v[:, :, c0:c0 + 512], hb[s][:], sds[s])
                    B.wait("pe", th, twf, fbp_free[s])
                    for k in range(KC):
                        mm = pe.matmul(ps[2 + s][0:8, :], lhsT=wfb[:, k, :], rhs=hb[s][:, k, :], start=(k == 0), stop=(k == KC - 1))
                    tf = B.sig("pe", mm)
                    hb_free[s] = [tst, tf]
                    B.wait("dve", tf)
                    tc = B.sig("dve", dve.tensor_copy(out=FBt[:, c0:c0 + 512], in_=ps[2 + s][0:8, :]))
                    fbp_free[s] = tc
                B.barrier()
            with ExitStack() as p0c:
                Ct = sb(p0c, "Ct", [8, S], F32)
                P3 = sb(p0c, "P3", [8, 3, S], BF16)
                te = B.sig("act", act.activation(out=FBt[:], in_=FBt[:], func=AF.Exp, bias=nbf[:, 0:1], scale=-1.0))
                B.wait("act", te)
                tln = B.sig("act", act.activation(out=FBt[:], in_=FBt[:], func=AF.Ln, bias=1.0, scale=1.0))
                B.wait("dve", tln)
                tsc = B.sig("dve", dve.tensor_tensor_scan(out=Ct[:], data0=FBt[:], data1=FBt[:], initial=0.0, op0=ALU.add, op1=ALU.bypass))
                for i in range(3):
                    B.wait("dve", B.last("dve"))
                    tcp = B.sig("dve", dve.tensor_copy(out=P3[:, i, :], in_=Ct[:]))
                    if i < 2:
                        B.wait("dve", tcp)
                        B.sig("dve", dve.tensor_tensor(out=Ct[:], in0=Ct[:], in1=P3[:, i, :], op=ALU.subtract))
                B.wait("sp", tcp)
                cds2 = B.dsem()
                tcd = B.dma("sp", cpos_d[:, :, :], P3[:], cds2)
                B.wait("dve", tcd)
                tng = B.sig("dve", dve.tensor_scalar(out=P3[:], in0=P3[:], scalar1=-1.0, scalar2=None, op0=ALU.mult))
                B.wait("sp", tng)
                B.dma("sp", cneg_d[:, :, :], P3[:], cds2)
                B.barrier()

        with ExitStack() as p1:
            qT = [sb(p1, "qT%d" % i, [70, S], BF16) for i in range(2)]
            kT = [sb(p1, "kT%d" % i, [70, S], BF16) for i in range(2)]
            Va = [sb(p1, "Va%d" % i, [128, NT, 128], BF16) for i in range(2)]
            sz = [sb(p1, "sz%d" % i, [64, S], BF16) for i in range(2)]
            hs = [sb(p1, "hs%d" % i, [128, KC, 512], BF16) for i in range(2)]
            wq = sb(p1, "wq", [128, KC, 128], BF16)
            wk = sb(p1, "wk", [128, KC, 128], BF16)
            wv = sb(p1, "wv", [128, KC, 128], BF16)
            wz = sb(p1, "wz", [128, KC, 128], BF16)
            Pt = [sb(p1, "Pt%d" % i, [128, 1024], BF16) for i in range(3)]
            rec = [sb(p1, "rec%d" % i, [64, 512], F32) for i in range(2)]
            ytmp = [sb(p1, "ytmp%d" % i, [64, 512], F32) for i in range(2)]
            yo = [sb(p1, "yo%d" % i, [64, 512], BF16) for i in range(2)]
            hds = [B.dsem() for _ in range(2)]
            wds = B.dsem()
            ads = B.dsem()
            yds = [B.dsem() for _ in range(2)]
            for i in range(2):
                B.sig("dve", dve.memset(qT[i][64:70, :], 1.0))
                B.sig("dve", dve.memset(kT[i][64:70, :], 1.0))
                B.sig("dve", dve.memset(Va[i][:, :, 64:128], 1.0))
            tms = B.last("dve")
            B.wait("sp", tms)
            B.wait("pe", tms)
            att_done = None
            nrm_done = None
            hs_free = [None, None]
            psb_free = {}
            yds_tok = [None, None]
            for hp in range(4):
                B.wait("sp", att_done, nrm_done)
                c = hp * 128
                B.dma("sp", wq[:], wbf_v[:, :, QB + c:QB + c + 128], wds)
                B.dma("sp", wk[:], wbf_v[:, :, KB + c:KB + c + 128], wds)
                B.dma("sp", wv[:], wbf_v[:, :, VB + c:VB + c + 128], wds)
                tw = B.dma("sp", wz[:], wbf_v[:, :, ZB + c:ZB + c + 128], wds)
                for i in range(2):
                    h = 2 * hp + i
                    B.dma("sp", qT[i][64:67, :], cneg_d[h, :, :], ads)
                    ta_ = B.dma("sp", kT[i][67:70, :], cpos_d[h, :, :], ads)
                for e in ("act", "dve"):
                    B.wait(e, att_done, nrm_done)
                B.wait("pe", tw, ta_, B.last("act"))
                ltok = {}

                def load_h(blk):
                    s_ = blk % 2
                    B.wait("sp", hs_free[s_])
                    ltok[blk] = B.dma("sp", hs[s_][:], hT_v[:, :, blk * 512:(blk + 1) * 512], hds[s_])
                load_h(0)
                for blk in range(NB):
                    s = blk % 2
                    c0 = blk * 512
                    if blk + 1 < NB:
                        load_h(blk + 1)
                    B.wait("pe", ltok[blk])
                    for (wt, bank, kind) in ((wq, 0, "q"), (wk, 1, "k"), (wz, 2, "z")):
                        B.wait("pe", psb_free.get(bank))
                        for k in range(KC):
                            mm = pe.matmul(ps[bank][:, :], lhsT=wt[:, k, :], rhs=hs[s][:, k, :], start=(k == 0), stop=(k == KC - 1))
                        tp = B.sig("pe", mm)
                        if kind == "q":
                            B.wait("act", tp)
                            B.sig("act", act.mul(qT[0][0:64, c0:c0 + 512], ps[bank][0:64, :], 0.125))
                            psb_free[bank] = B.sig("act", act.mul(qT[1][0:64, c0:c0 + 512], ps[bank][64:128, :], 0.125))
                        elif kind == "k":
                            B.wait("dve", tp)
                            B.sig("dve", dve.tensor_copy(out=kT[0][0:64, c0:c0 + 512], in_=ps[bank][0:64, :]))
                            psb_free[bank] = B.sig("dve", dve.tensor_copy(out=kT[1][0:64, c0:c0 + 512], in_=ps[bank][64:128, :]))
                        else:
                            B.wait("act", tp)
                            B.sig("act", act.activation(out=sz[0][:, c0:c0 + 512], in_=ps[bank][0:64, :], func=AF.Silu))
                            psb_free[bank] = B.sig("act", act.activation(out=sz[1][:, c0:c0 + 512], in_=ps[bank][64:128, :], func=AF.Silu))
                    B.wait("pe", psb_free.get(3))
                    for tt in range(4):
                        for k in range(KC):
                            mm = pe.matmul(ps[3][:, tt * 128:(tt + 1) * 128], lhsT=hs[s][:, k, tt * 128:(tt + 1) * 128], rhs=wv[:, k, :], start=(k == 0), stop=(k == KC - 1))
                    tp = B.sig("pe", mm)
                    hs_free[s] = tp
                    B.wait("dve", tp)
                    pv = ps[3][:, :].rearrange("p (t c) -> p t c", c=128)
                    B.sig("dve", dve.tensor_copy(out=Va[0][:, blk * 4:blk * 4 + 4, 0:64], in_=pv[:, :, 0:64]))
                    psb_free[3] = B.sig("dve", dve.tensor_copy(out=Va[1][:, blk * 4:blk * 4 + 4, 0:64], in_=pv[:, :, 64:128]))
                proj_done = [B.last("act"), B.last("dve")]
                B.wait("pe", *proj_done)
                steps = []
                for i in range(2):
                    for qb in range(NB):
                        nj = 4 * qb + 4
                        for j in range(0, 4 * qb, 2):
                            steps.append((i, qb, [j, j + 1], nj))
                        for j in range(4 * qb, nj):
                            steps.append((i, qb, [j], nj))
                sbank_free = [None, None]
                pt_free = [None, None, None]
                acc_free = [None, None]
                exp_tok = [None] * len(steps)
                accn = [0]
                acc_of = {}

                def emit_qk(n):
                    i, qb, js, nj = steps[n]
                    sp_ = psA[n % 2]
                    B.wait("pe", sbank_free[n % 2])
                    for t_, j in enumerate(js):
                        m = j - 4 * qb
                        cc = 128 * m if m >= 0 else 0
                        o = 512 * t_
                        mm = pe.matmul(sp_[:, o + cc:o + 512], lhsT=kT[i][:, j * 128:(j + 1) * 128], rhs=qT[i][:, qb * 512 + cc:(qb + 1) * 512], start=True, stop=(m < 0))
                        if m >= 0:
                            mm = pe.matmul(sp_[:, o + cc:o + cc + 128], lhsT=ident_bf[:], rhs=mcur_bf[:], start=False, stop=True)
                    tp = B.sig("pe", mm)
                    lo = cc if len(js) == 1 else 0
                    hi = 512 * len(js)
                    B.wait("act", tp, pt_free[n % 3])
                    te_ = B.sig("act", act.activation(out=Pt[n % 3][:, lo:hi], in_=sp_[:, lo:hi], func=AF.Exp))
                    sbank_free[n % 2] = te_
                    exp_tok[n] = te_

                def emit_pv(n):
                    i, qb, js, nj = steps[n]
                    if js[0] == 0:
                        acc_of[(i, qb)] = accn[0] % 2
                        accn[0] += 1
                        B.wait("pe", acc_free[acc_of[(i, qb)]])
                    a = acc_of[(i, qb)]
                    B.wait("pe", exp_tok[n])
                    for t_, j in enumerate(js):
                        m = j - 4 * qb
                        cc = 128 * m if m >= 0 else 0
                        o = 512 * t_
                        mm = pe.matmul(ps[6 + a][:, cc:512], lhsT=Va[i][:, j, :], rhs=Pt[n % 3][:, o + cc:o + 512], start=(j == 0), stop=(j == nj - 1))
                    tp = B.sig("pe", mm)
                    pt_free[n % 3] = tp
                    if js[-1] == nj - 1:
                        h = 2 * hp + i
                        B.wait("dve", tp, yds_tok[a])
                        t1 = B.sig("dve", dve.reciprocal(out=rec[a][:], in_=ps[6 + a][64:128, :]))
                        B.wait("dve", t1)
                        t2 = B.sig("dve", dve.tensor_tensor(out=ytmp[a][:], in0=ps[6 + a][0:64, :], in1=rec[a][:], op=ALU.mult))
                        acc_free[a] = t2
                        B.wait("dve", t2)
                        t3 = B.sig("dve", dve.tensor_tensor(out=yo[a][:], in0=ytmp[a][:], in1=sz[i][:, qb * 512:(qb + 1) * 512], op=ALU.mult))
                        B.wait("sp", t3)
                        yds_tok[a] = B.dma("sp", ygb_d[h * 64:(h + 1) * 64, qb * 512:(qb + 1) * 512], yo[a][:], yds[a])

                for n in range(len(steps) + 1):
                    if n < len(steps):
                        emit_qk(n)
                    if n >= 1:
                        emit_pv(n - 1)
                att_done = B.last("pe")
                nrm_done = B.last("dve")
            B.barrier()

        with ExitStack() as pmid:
            cosT = sb(pmid, "cosT", [32, S], BF16)
            sinT = sb(pmid, "sinT", [32, S], BF16)
            with ExitStack() as p0:
                CW = min(2048, S)
                posi = sb(p0, "posi", [64, CW], I32)
                ang = sb(p0, "ang", [64, CW], F32)
                tmpf = sb(p0, "tmpf", [64, CW], F32)
                tmpi = sb(p0, "tmpi", [64, CW], I32)
                lds = B.dsem()
                prev = None
                for c0 in range(0, S, CW):
                    B.wait("sp", prev)
                    tl = B.dma("sp", posi[:], posr_d[:, c0:c0 + CW], lds)
                    B.wait("dve", tl, prev)

                    def dv(inst):
                        t = B.sig("dve", inst)
                        B.wait("dve", t)
                        return t
                    dv(dve.tensor_copy(out=ang[:], in_=posi[:]))
                    dv(dve.tensor_scalar(out=ang[:], in0=ang[:], scalar1=cinv[:, 0:1], scalar2=cinv[:, 1:2], op0=ALU.mult, op1=ALU.add))
                    dv(dve.tensor_scalar(out=tmpf[:], in0=ang[:], scalar1=1.0 / TWO_PI, scalar2=None, op0=ALU.mult))
                    dv(dve.tensor_copy(out=tmpi[:], in_=tmpf[:]))
                    dv(dve.tensor_copy(out=tmpf[:], in_=tmpi[:]))
                    dv(dve.scalar_tensor_tensor(out=ang[:], in0=tmpf[:], scalar=-TWO_PI, in1=ang[:], op0=ALU.mult, op1=ALU.add))
                    dv(dve.tensor_scalar(out=tmpf[:], in0=ang[:], scalar1=PI, scalar2=-TWO_PI, op0=ALU.is_gt, op1=ALU.mult))
                    dv(dve.tensor_tensor(out=ang[:], in0=ang[:], in1=tmpf[:], op=ALU.add))
                    dv(dve.tensor_scalar(out=tmpf[:], in0=ang[:], scalar1=-PI, scalar2=TWO_PI, op0=ALU.is_lt, op1=ALU.mult))
                    dv(dve.tensor_tensor(out=ang[:], in0=ang[:], in1=tmpf[:], op=ALU.add))
                    td = dv(dve.tensor_scalar(out=ang[:], in0=ang[:], scalar1=PI, scalar2=-PI, op0=ALU.min, op1=ALU.max))
                    B.wait("act", td)
                    B.sig("act", act.activation(out=sinT[:, c0:c0 + CW], in_=ang[0:32, :], func=AF.Sin))
                    prev = B.sig("act", act.activation(out=cosT[:, c0:c0 + CW], in_=ang[32:64, :], func=AF.Sin))
                B.barrier()

            with ExitStack() as p2:
                qT2 = sb(p2, "qT2", [128, S], BF16)
                kT2 = sb(p2, "kT2", [128, S], BF16)
                vT2 = sb(p2, "vT2", [128, S], BF16)
                NDA = sb(p2, "NDA", [128, 2, S], BF16)
                hs = [sb(p2, "hs2_%d" % i, [128, KC, 512], BF16) for i in range(2)]
                w6 = [sb(p2, "w6_%d" % i, [128, KC, 128], BF16) for i in range(3)]
                Vc = [sb(p2, "Vc%d" % i, [128, 128], BF16) for i in range(48)]
                P2 = [sb(p2, "P2_%d" % i, [128, 256], BF16) for i in range(2)]
                rt1 = [sb(p2, "rt1_%d" % i, [32, 512], F32) for i in range(2)]
                rt2 = [sb(p2, "rt2_%d" % i, [32, 512], F32) for i in range(2)]
                sza = [sb(p2, "sza%d" % i, [128, 512], BF16) for i in range(2)]
                szm = [sb(p2, "szm%d" % i, [128, 512], BF16) for i in range(2)]
                qm = [sb(p2, "qm%d" % i, [128, 512], BF16) for i in range(2)]
                Pm = [sb(p2, "Pm%d" % i, [128, 2, 512], BF16) for i in range(2)]
                recm = [sb(p2, "recm%d" % i, [128, 512], F32) for i in range(2)]
                tmpm = [sb(p2, "tmpm%d" % i, [128, 512], F32) for i in range(2)]
                yoa = [sb(p2, "yoa%d" % i, [128, 512], BF16) for i in range(2)]
                yom = [sb(p2, "yom%d" % i, [128, 512], BF16) for i in range(2)]
                hds = [B.dsem() for _ in range(2)]
                wds = B.dsem()
                yads = [B.dsem() for _ in range(2)]
                ymds = [B.dsem() for _ in range(2)]
                SC = float(128.0 ** -0.5)
                B.wait("sp", wtok("wA"), wtok("wM"))
                hs_free = [None, None]
                sweep_done = None
                ya_tok = [None, None]
                ym_tok = [None, None]
                for hh in range(4):
                    for sweep in range(4):
                        B.wait("sp", sweep_done)
                        for e in ("pe", "act", "dve", "pool"):
                            B.wait(e, sweep_done)
                        if sweep < 3:
                            Hc = (sweep * 4 + hh) * 128
                            cols = [QA + Hc, KA + Hc, VA + Hc]
                        else:
                            cols = [ZA + hh * 128, QM + hh * 128, ZM + hh * 128]
                        for wi, cc in enumerate(cols):
                            tw = B.dma("sp", w6[wi][:], wbf_v[:, :, cc:cc + 128], wds)
                        B.wait("pe", tw)
                        bank_free = {}
                        ltok = {}

                        def load_h2(blk):
                            s_ = blk % 2
                            B.wait("sp", hs_free[s_])
                            ltok[blk] = B.dma("sp", hs[s_][:], hT_v[:, :, blk * 512:(blk + 1) * 512], hds[s_])
                        load_h2(0)
                        if sweep < 3:
                            dl = DILS[sweep]
                            nbk = S // (128 * dl)
                            ready_at = {}
                            for r_ in range(dl):
                                for n_ in range(nbk):
                                    ready_at.setdefault((r_ + (128 * n_ + 127) * dl) // 512, []).append((r_, n_))
                            blocks = [cb for b_ in range(NB) for cb in ready_at.get(b_, [])]
                            bmax_of = [(r_ + (128 * n_ + 127) * dl) // 512 for (r_, n_) in blocks]
                            NBK_ = len(blocks)
                            seq = []
                            for it in range(NBK_ + 1):
                                if it < NBK_:
                                    seq.append(("A", it))
                                if it >= 1:
                                    seq.append(("B", it - 1))
                            seq_pos = [0]
                            s2_free = [None, None]
                            p2_free = [None, None]
                            nd_free = [None, None]
                            pv_tok = {}
                            tvc_tok = [None] * NBK_
                            exp_tok2 = [None] * NBK_
                            rdy = {}

                            def cls_(r, nn):
                                st = r + 128 * nn * dl
                                return slice(st, st + 127 * dl + 1, dl)

                            def stage_a(i):
                                r, n = blocks[i]
                                u = i % 2
                                vcur = Vc[3 * r + (n % 3)]
                                tpb = ps[6 + u][:, 256:512].bitcast(BF16)[:, 0:128]
                                s2 = ps[4 + u][:, 0:256]
                                B.wait("pe", nd_free[u], rdy[bmax_of[i]])
                                ttp = B.sig("pe", pe.transpose(out=tpb, in_=vT2[:, cls_(r, n)], identity=ident_bf[:]))
                                B.wait("dve", ttp, pv_tok.get((r, n - 2)))
                                tvc_tok[i] = B.sig("dve", dve.tensor_copy(out=vcur[:], in_=tpb))
                                W = 256 if n > 0 else 128
                                B.wait("pe", s2_free[u])
                                pe.matmul(s2[:, 0:128], lhsT=kT2[:, cls_(r, n)], rhs=qT2[:, cls_(r, n)], start=True, stop=False)
                                mm = pe.matmul(s2[:, 0:128], lhsT=ident_bf[:], rhs=mcur_bf[:], start=False, stop=True)
                                if n > 0:
                                    pe.matmul(s2[:, 128:256], lhsT=kT2[:, cls_(r, n - 1)], rhs=qT2[:, cls_(r, n)], start=True, stop=False)
                                    mm = pe.matmul(s2[:, 128:256], lhsT=ident_bf[:], rhs=mprev_bf[:], start=False, stop=True)
                                ts2 = B.sig("pe", mm)
                                B.wait("act", ts2, p2_free[u])
                                tex = B.sig("act", act.activation(out=P2[u][:, 0:W], in_=s2[:, 0:W], func=AF.Exp, scale=SC))
                                s2_free[u] = tex
                                exp_tok2[i] = tex

                            def stage_b(i):
                                r, n = blocks[i]
                                u = i % 2
                                vcur = Vc[3 * r + (n % 3)]
                                vprv = Vc[3 * r + ((n - 1) % 3)]
                                nd = ps[6 + u]
                                B.wait("pe", exp_tok2[i], tvc_tok[i])
                                pe.matmul(nd[:, 0:128], lhsT=vcur[:], rhs=P2[u][:, 0:128], start=True, stop=(n == 0))
                                if n > 0:
                                    pe.matmul(nd[:, 0:128], lhsT=vprv[:], rhs=P2[u][:, 128:256], start=False, stop=True)
                                mm = pe.matmul(nd[:, 128:256], lhsT=ones_bf[:], rhs=P2[u][:, 0:128], start=True, stop=(n == 0))
                                if n > 0:
                                    mm = pe.matmul(nd[:, 128:256], lhsT=ones_bf[:], rhs=P2[u][:, 128:256], start=False, stop=True)
                                tnd = B.sig("pe", mm)
                                p2_free[u] = tnd
                                pv_tok[(r, n)] = tnd
                                B.wait("dve", tnd)
                                ndv = nd[:, 0:256].rearrange("p (a c) -> p a c", a=2)
                                if sweep == 0:
                                    tac = B.sig("dve", dve.tensor_copy(out=NDA[:, :, cls_(r, n)], in_=ndv))
                                else:
                                    tac = B.sig("dve", dve.tensor_tensor(out=NDA[:, :, cls_(r, n)], in0=ndv, in1=NDA[:, :, cls_(r, n)], op=ALU.add))
                                nd_free[u] = tac

                            def pump(upto_blk, rate):
                                cnt_ = 0
                                while seq_pos[0] < len(seq) and cnt_ < rate:
                                    kind, i = seq[seq_pos[0]]
                                    if bmax_of[i] > upto_blk:
                                        break
                                    if kind == "A":
                                        stage_a(i)
                                    else:
                                        stage_b(i)
                                    seq_pos[0] += 1
                                    cnt_ += 1
                        for blk in range(NB):
                            s = blk % 2
                            c0 = blk * 512
                            if blk + 1 < NB:
                                load_h2(blk + 1)
                            B.wait("pe", ltok[blk])
                            if sweep < 3:
                                tps = []
                                evs = []
                                for wi in range(3):
                                    B.wait("pe", bank_free.get(wi))
                                    for k in range(KC):
                                        mm = pe.matmul(ps[wi][:, :], lhsT=w6[wi][:, k, :], rhs=hs[s][:, k, :], start=(k == 0), stop=(k == KC - 1))
                                    tp_ = B.sig("pe", mm)
                                    tps.append(tp_)
                                    if wi == 0:
                                        B.wait("act", tp_)
                                        tq = B.sig("act", act.copy(out=qT2[:, c0:c0 + 512], in_=ps[0][:, :]))
                                        bank_free[0] = tq
                                    elif wi == 1:
                                        B.wait("dve", tp_)
                                        tk_ = B.sig("dve", dve.tensor_copy(out=kT2[:, c0:c0 + 512], in_=ps[1][:, :]))
                                        bank_free[1] = tk_
                                    else:
                                        B.wait("act", tp_)
                                        tv_ = B.sig("act", act.copy(out=vT2[:, c0:c0 + 512], in_=ps[2][:, :]))
                                        bank_free[2] = tv_
                                    pump(blk - 1, 3)
                                hs_free[s] = tps[-1]
                                rl = [tv_]
                                for ri_, (tt_, traw) in enumerate(((qT2, tq), (kT2, tk_))):
                                    bk = 3
                                    B.wait("pe", traw, bank_free.get(bk))
                                    tsw = B.sig("pe", pe.matmul(ps[bk][0:32, :], lhsT=perm_bf[:], rhs=tt_[0:32, c0:c0 + 512], start=True, stop=True))
                                    B.wait("pool", traw, bank_free.get(("rt1", ri_)))
                                    tp1 = B.sig("pool", pool.tensor_tensor(out=rt1[ri_][:], in0=tt_[0:32, c0:c0 + 512], in1=cosT[:, c0:c0 + 512], op=ALU.mult))
                                    B.wait("dve", tsw, bank_free.get(("rt2", ri_)))
                                    tp2 = B.sig("dve", dve.tensor_tensor(out=rt2[ri_][:], in0=ps[bk][0:32, :], in1=sinT[:, c0:c0 + 512], op=ALU.mult))
                                    bank_free[bk] = tp2
                                    B.wait("pool", tp1, tp2, tsw)
                                    tp3 = B.sig("pool", pool.tensor_tensor(out=tt_[0:32, c0:c0 + 512], in0=rt1[ri_][:], in1=rt2[ri_][:], op=ALU.add))
                                    bank_free[("rt1", ri_)] = tp3
                                    bank_free[("rt2", ri_)] = tp3
                                    rl.append(tp3)
                                    pump(blk - 1, 2)
                                rdy[blk] = rl
                            else:
                                tps = []
                                for wi in range(3):
                                    B.wait("pe", bank_free.get(wi))
                                    for k in range(KC):
                                        mm = pe.matmul(ps[wi][:, :], lhsT=w6[wi][:, k, :], rhs=hs[s][:, k, :], start=(k == 0), stop=(k == KC - 1))
                                    tps.append(B.sig("pe", mm))
                                hs_free[s] = tps[-1]
                                a = blk % 2
                                B.wait("act", tps[0], bank_free.get(("sza", a)))
                                tz = B.sig("act", act.activation(out=sza[a][:], in_=ps[0][:, :], func=AF.Silu))
                                bank_free[0] = tz
                                B.wait("dve", bank_free.get(("recm", a)))
                                t1 = B.sig("dve", dve.reciprocal(out=recm[a][:], in_=NDA[:, 1, c0:c0 + 512]))
                                B.wait("dve", t1)
                                t2 = B.sig("dve", dve.tensor_tensor(out=tmpm[a][:], in0=NDA[:, 0, c0:c0 + 512], in1=recm[a][:], op=ALU.mult))
                                B.wait("dve", t2, tz, ya_tok[a])
                                t3 = B.sig("dve", dve.tensor_tensor(out=yoa[a][:], in0=tmpm[a][:], in1=sza[a][:], op=ALU.mult))
                                bank_free[("sza", a)] = t3
                                B.wait("sp", t3)
                                ya_tok[a] = B.dma("sp", yga_d[hh * 128:(hh + 1) * 128, c0:c0 + 512], yoa[a][:], yads[a])
                                B.wait("act", tps[1], bank_free.get(("qm", a)))
                                tqm = B.sig("act", act.copy(out=qm[a][:], in_=ps[1][:, :]))
                                bank_free[1] = tqm
                                B.wait("act", tps[2], bank_free.get(("szm", a)))
                                tzm = B.sig("act", act.activation(out=szm[a][:], in_=ps[2][:, :], func=AF.Silu))
                                bank_free[2] = tzm
                                B.wait("pe", tqm)
                                for kt in range(2):
                                    B.wait("pe", bank_free.get(3 + kt))
                                    tsm = B.sig("pe", pe.matmul(ps[3 + kt][:, :], lhsT=KmT[:, hh, kt * 128:(kt + 1) * 128], rhs=qm[a][:], start=True, stop=True))
                                    B.wait("act", tsm, bank_free.get(("Pm", a)))
                                    bank_free[3 + kt] = B.sig("act", act.activation(out=Pm[a][:, kt, :], in_=ps[3 + kt][:, :], func=AF.Exp, scale=SC))
                                texp = B.last("act")
                                bank_free[("qm", a)] = tsm
                                B.wait("pe", texp, bank_free.get(5), bank_free.get(6))
                                for kt in range(2):
                                    mm = pe.matmul(ps[5][:, :], lhsT=Vm[:, kt, hh * 128:(hh + 1) * 128], rhs=Pm[a][:, kt, :], start=(kt == 0), stop=(kt == 1))
                                for kt in range(2):
                                    mm = pe.matmul(ps[6][:, :], lhsT=ones_bf[:], rhs=Pm[a][:, kt, :], start=(kt == 0), stop=(kt == 1))
                                tpv = B.sig("pe", mm)
                                bank_free[("Pm", a)] = tpv
                                B.wait("dve", tpv, t3)
                                t1 = B.sig("dve", dve.reciprocal(out=recm[a][:], in_=ps[6][:, :]))
                                bank_free[6] = t1
                                B.wait("dve", t1)
                                t2 = B.sig("dve", dve.tensor_tensor(out=tmpm[a][:], in0=ps[5][:, :], in1=recm[a][:], op=ALU.mult))
                                bank_free[5] = t2
                                B.wait("dve", t2, tzm, ym_tok[a])
                                t3 = B.sig("dve", dve.tensor_tensor(out=yom[a][:], in0=tmpm[a][:], in1=szm[a][:], op=ALU.mult))
                                bank_free[("szm", a)] = t3
                                bank_free[("recm", a)] = t3
                                B.wait("sp", t3)
                                ym_tok[a] = B.dma("sp", ygm_d[hh * 128:(hh + 1) * 128, c0:c0 + 512], yom[a][:], ymds[a])
                        if sweep < 3:
                            pump(NB, 10 ** 9)
                        sweep_done = [B.last("pe"), B.last("act"), B.last("dve"), B.last("pool")]
                B.barrier()

        with ExitStack() as p3:
            Wgl = sb(p3, "Wgl", [128, KC, 3072], BF16)
            Wbr = [sb(p3, "Wbr%d" % i, [128, 4, D], BF16) for i in range(3)]
            Wo = sb(p3, "Wo", [128, KC, D], BF16)
            hs = [sb(p3, "hs3_%d" % i, [128, KC, 512], BF16) for i in range(2)]
            yg = [sb(p3, "yg%d" % b_, [128, 4, 512], BF16) for b_ in range(3)]
            G = sb(p3, "G", [128, 24, 512], BF16)
            mg = sb(p3, "mg", [128, KC, 512], BF16)
            m1 = sb(p3, "m1", [128, 512], F32)
            m2 = sb(p3, "m2", [128, 512], F32)
            m3 = sb(p3, "m3", [128, 512], F32)
            xtok = [sb(p3, "xtok%d" % i, [128, D], F32) for i in range(2)]
            ytok = [sb(p3, "ytok%d" % i, [128, D], F32) for i in range(2)]
            junk = sb(p3, "junk", [128, 512], BF16)
            ssq = [sb(p3, "ssq%d" % i, [128, 4], F32) for i in range(2)]
            lds = [B.dsem() for _ in range(2)]
            ygds = B.dsem()
            xds = [B.dsem() for _ in range(2)]
            ods = [B.dsem() for _ in range(2)]
            wds = B.dsem()
            B.wait("sp", wtok("wG"), wtok("wa"), wtok("wb"), wtok("wm"), wtok("wout"))
            for k in range(KC):
                B.dma("sp", Wgl[:, k, :], wbf_d[k * 128:(k + 1) * 128, GL:GL + 3072], wds)
            for b_, wd in enumerate((wabf_d, wbbf_d, wmbf_d)):
                B.dma("sp", Wbr[b_][:], wd.rearrange("(e p) c -> p e c", p=128), wds)
            tw = B.dma("sp", Wo[:], woutbf_d.rearrange("(k p) c -> p k c", p=128), wds)
            B.wait("pe", tw)
            ygd = (yga_d, ygb_d, ygm_d)
            hs_free = [None, None]
            yg_free = None
            G_free = None
            mg_free = None
            bank_free = {}
            xt_free = [None, None]
            yt_free = [None, None]
            tcnt = 0
            ltok = {}

            def load_h3(blk):
                s_ = blk % 2
                B.wait("sp", hs_free[s_])
                ltok[blk] = B.dma("sp", hs[s_][:], hT_v[:, :, blk * 512:(blk + 1) * 512], lds[s_])
            load_h3(0)
            xtk = {}

            def load_xt(ti):
                a_ = ti % 2
                B.wait("sp", xt_free[a_])
                xtk[ti] = B.dma("sp", xtok[a_][:], x_d[ti * 128:(ti + 1) * 128, :], xds[a_])
            load_xt(0)
            st3 = {"yg_free": None, "G_free": None, "mg_free": None, "tyg": None, "tmg": None}

            def load_yg(blk):
                c0 = blk * 512
                B.wait("sp", st3["yg_free"])
                for b_ in range(3):
                    st3["tyg"] = B.dma("sp", yg[b_][:], ygd[b_].rearrange("(e p) s -> p e s", p=128)[:, :, c0:c0 + 512], ygds)

            def gates(blk, lo, hi):
                s = blk % 2
                if lo == 0:
                    B.wait("pe", ltok[blk])
                    B.wait("act", st3["G_free"])
                for gi in range(lo, hi):
                    bk = gi % 2
                    B.wait("pe", bank_free.get(bk))
                    for k in range(KC):
                        mm = pe.matmul(ps[bk][:, :], lhsT=Wgl[:, k, gi * 128:(gi + 1) * 128], rhs=hs[s][:, k, :], start=(k == 0), stop=(k == KC - 1))
                    tp = B.sig("pe", mm)
                    B.wait("act", tp)
                    bank_free[bk] = B.sig("act", act.activation(out=G[:, gi, :], in_=ps[bk][:, :], func=AF.Sigmoid, bias=bmerge[:, gi:gi + 1]))
                if hi == 24:
                    st3["tg"] = B.last("act")
                    hs_free[s] = B.last("pe")

            def branch(blk):
                B.wait("dve", st3["tg"])
                B.wait("pool", st3["mg_free"])
                B.wait("pe", st3["tyg"])
                for c in range(KC):
                    tb = []
                    for b_ in range(3):
                        bk = 2 + b_
                        B.wait("pe", bank_free.get(bk))
                        for e_ in range(4):
                            mm = pe.matmul(ps[bk][:, :], lhsT=Wbr[b_][:, e_, c * 128:(c + 1) * 128], rhs=yg[b_][:, e_, :], start=(e_ == 0), stop=(e_ == 3))
                        tb.append(B.sig("pe", mm))
                    mt = (m1, m2, m3)
                    tms_ = []
                    for b_ in range(3):
                        B.wait("dve", tb[b_], bank_free.get("m"))
                        t_ = B.sig("dve", dve.tensor_tensor(out=mt[b_][:], in0=ps[2 + b_][:, :], in1=G[:, b_ * 8 + c, :], op=ALU.mult))
                        bank_free[2 + b_] = t_
                        tms_.append(t_)
                    B.wait("pool", *tms_)
                    ta1 = B.sig("pool", pool.tensor_tensor(out=m1[:], in0=m1[:], in1=m2[:], op=ALU.add))
                    B.wait("pool", ta1)
                    ta2 = B.sig("pool", pool.tensor_tensor(out=mg[:, c, :], in0=m1[:], in1=m3[:], op=ALU.add))
                    bank_free["m"] = ta2
                st3["G_free"] = B.last("dve")
                st3["yg_free"] = B.last("pe")
                st3["tmg"] = B.last("pool")

            def outproj(blk, tt):
                ti = blk * 4 + tt
                a = ti % 2
                r0 = ti * 128
                if ti + 1 < NT:
                    load_xt(ti + 1)
                tx = xtk[ti]
                B.wait("pe", st3["tmg"])
                for half in range(2):
                    bk = 5 + half
                    B.wait("pe", bank_free.get(bk))
                    for k in range(KC):
                        mm = pe.matmul(ps[bk][:, :], lhsT=mg[:, k, tt * 128:(tt + 1) * 128], rhs=Wo[:, k, half * 512:(half + 1) * 512], start=(k == 0), stop=(k == KC - 1))
                to = B.sig("pe", mm)
                if tt == 3:
                    st3["mg_free"] = to
                B.wait("act", to, bank_free.get(("ssq", a)))
                for half in range(2):
                    tsq = B.sig("act", act.activation(out=junk[:], in_=ps[5 + half][:, :], func=AF.Square, accum_out=ssq[a][:, half:half + 1]))
                B.wait("dve", tsq)
                t1 = B.sig("dve", dve.tensor_tensor(out=ssq[a][:, 2:3], in0=ssq[a][:, 0:1], in1=ssq[a][:, 1:2], op=ALU.add))
                B.wait("dve", t1)
                t2 = B.sig("dve", dve.tensor_scalar(out=ssq[a][:, 2:3], in0=ssq[a][:, 2:3], scalar1=1.0 / D, scalar2=EPS, op0=ALU.mult, op1=ALU.add))
                B.wait("act", t2)
                t3 = B.sig("act", act.activation(out=ssq[a][:, 3:4], in_=ssq[a][:, 2:3], func=AF.Sqrt))
                B.wait("dve", t3)
                t4 = B.sig("dve", dve.reciprocal(out=ssq[a][:, 3:4], in_=ssq[a][:, 3:4]))
                B.wait("dve", t4, yt_free[a])
                for half in range(2):
                    t5 = B.sig("dve", dve.scalar_tensor_tensor(out=ytok[a][:, half * 512:(half + 1) * 512], in0=ps[5 + half][:, :], scalar=ssq[a][:, 3:4], in1=gpost[:, half * 512:(half + 1) * 512], op0=ALU.mult, op1=ALU.mult))
                    bank_free[5 + half] = t5
                bank_free[("ssq", a)] = t5
                B.wait("pool", t5, tx)
                t6 = B.sig("pool", pool.tensor_tensor(out=ytok[a][:], in0=ytok[a][:], in1=xtok[a][:], op=ALU.add))
                xt_free[a] = t6
                B.wait("sp", t6)
                yt_free[a] = B.dma("sp", y_d[r0:r0 + 128, :], ytok[a][:], ods[a])

            load_yg(0)
            if NB > 1:
                load_h3(1)
            gates(0, 0, 24)
            branch(0)
            for blk in range(NB):
                nxt = blk + 1 < NB
                if nxt:
                    load_yg(blk + 1)
                    if blk + 2 < NB:
                        pass
                for tt in range(4):
                    if nxt:
                        if tt == 0 and blk + 2 < NB:
                            pass
                        gates(blk + 1, 6 * tt, 6 * tt + 6)
                        if tt == 3 and blk + 2 < NB:
                            load_h3(blk + 2)
                    outproj(blk, tt)
                if nxt:
                    branch(blk + 1)
            B.barrier()
    return nc


def _consts():
    ident = np.eye(128, dtype=np.float32)
    p = np.arange(128)[:, None]
    f = np.arange(128)[None, :]
    mcur = np.where(p <= f, 0.0, NEGM).astype(np.float32)
    mprev = np.where(p >= f, 0.0, NEGM).astype(np.float32)
    perm = np.zeros((32, 32), np.float32)
    for m in range(16):
        perm[m + 16, m] = -1.0
        perm[m, m + 16] = 1.0
    half = 16
    inv = (np.float32(500000.0) ** (-np.arange(half, dtype=np.float32) / np.float32(half))).astype(np.float32)
    cinv = np.zeros((64, 2), np.float32)
    for q in range(64):
        cinv[q, 0] = inv[q % 16]
        cinv[q, 1] = 0.0 if q < 32 else np.float32(np.pi / 2)
    return ident, mcur, mprev, perm, cinv


def make_in_map(b, S, x, mem, positions, norm_pre_g, norm_post_g, norm_mem_g, w_in, b_forget, b_merge,
                w_mem_kv, w_branch_a, w_branch_b, w_branch_m, w_out):
    ident, mcur, mprev, perm, cinv = _consts()
    f32 = np.float32
    return {
        "xT": np.ascontiguousarray(x[b].T, dtype=f32),
        "x": np.ascontiguousarray(x[b], dtype=f32),
        "memT": np.ascontiguousarray(mem[b].T, dtype=f32),
        "posr": np.ascontiguousarray(np.broadcast_to(positions[b][None, :], (64, S)), dtype=np.int32),
        "w_in": np.ascontiguousarray(w_in[0], dtype=f32),
        "w_mem": np.ascontiguousarray(w_mem_kv[0], dtype=f32),
        "w_a": np.ascontiguousarray(w_branch_a[0], dtype=f32),
        "w_b": np.ascontiguousarray(w_branch_b[0], dtype=f32),
        "w_m": np.ascontiguousarray(w_branch_m[0], dtype=f32),
        "w_out": np.ascontiguousarray(w_out[0], dtype=f32),
        "gpre": np.ascontiguousarray(norm_pre_g[0].reshape(KC, 128).T, dtype=f32),
        "gmem": np.ascontiguousarray(norm_mem_g[0].reshape(KC, 128).T, dtype=f32),
        "gpost": np.ascontiguousarray(np.broadcast_to(norm_post_g[0][None, :], (128, D)), dtype=f32),
        "bmerge": np.ascontiguousarray(b_merge[0].reshape(24, 128).T, dtype=f32),
        "bfg": np.ascontiguousarray(b_forget[0].reshape(8, 1), dtype=f32),
        "c_ident": ident, "c_mcur": mcur, "c_mprev": mprev, "c_perm": perm, "c_inv": cinv,
    }


def kernel(**inputs):
    inputs = {k: np.asarray(v) for k, v in inputs.items()}
    x = inputs["x"]
    Bsz, S, _ = x.shape
    nc = build_program(S)
    in_maps = [make_in_map(b, S, **inputs) for b in range(Bsz)]
    res = run_bass_kernel_spmd(nc, in_maps, core_ids=list(range(Bsz)))
    out = np.stack([np.asarray(r["y"], dtype=np.float32) for r in res.results], axis=0)
    return out
```

```python
import numpy as np
from contextlib import ExitStack
import concourse.bass as bass
import concourse.mybir as mybir
from concourse.bass_utils import run_bass_kernel_spmd

F32 = mybir.dt.float32
BF16 = mybir.dt.bfloat16
I32 = mybir.dt.int32
AF = mybir.ActivationFunctionType
ALU = mybir.AluOpType

D = 1024
KC = 8
NMEM = 256
IN_COLS = 11272
QA, KA, VA, ZA = 0, 1536, 3072, 4608
QB, KB, VB, FBO, ZB = 5120, 5632, 6144, 6656, 6664
QM, ZM, GL = 7176, 7688, 8200
EPS = 1e-6
NEGM = -30000.0
TWO_PI = float(2.0 * np.pi)
PI = float(np.pi)
DILS = (1, 4, 16)


class DSem:
    def __init__(self, sem):
        self.sem = sem
        self.n = 0


class Bld:
    def __init__(self, nc, es):
        self.nc = nc
        self.es = es
        self.eng = {"pe": nc.tensor, "act": nc.scalar, "dve": nc.vector, "pool": nc.gpsimd, "sp": nc.sync}
        self.sem = {e: es.enter_context(nc.semaphore("s_" + e)) for e in ("pe", "act", "dve", "pool")}
        self.cnt = {e: 0 for e in self.sem}
        self.seen = {e: {} for e in self.eng}
        self.dsems = []
        self.nds = 0

    def sig(self, e, inst):
        inst.then_inc(self.sem[e], 1)
        self.cnt[e] += 1
        return (self.sem[e], self.cnt[e])

    def last(self, e):
        return (self.sem[e], self.cnt[e])

    def wait(self, e, *toks):
        for t in toks:
            if t is None:
                continue
            if isinstance(t, list):
                self.wait(e, *t)
                continue
            sem, v = t
            if v <= 0:
                continue
            k = id(sem)
            if self.seen[e].get(k, 0) >= v:
                continue
            self.eng[e].wait_ge(sem, v)
            self.seen[e][k] = v

    def dsem(self, track=True):
        s = self.es.enter_context(self.nc.semaphore("d%d" % self.nds))
        self.nds += 1
        d = DSem(s)
        if track:
            self.dsems.append(d)
        return d

    def dma(self, q, out, in_, ds):
        inst = self.eng[q].dma_start(out=out, in_=in_)
        ds.n += 16
        inst.then_inc(ds.sem, 16)
        return (ds.sem, ds.n)

    def barrier(self):
        toks = [self.last(e) for e in self.sem] + [(d.sem, d.n) for d in self.dsems]
        for e in self.eng:
            self.wait(e, *toks)


def build_program(S):
    NB = S // 512
    NT = S // 128
    nc = bass.Bass("TRN2", target_bir_lowering=False)

    def din(name, shape, dt=F32):
        return nc.dram_tensor(name, shape, dt, kind="ExternalInput").ap()

    def dscr(name, shape, dt):
        return nc.dram_tensor(name, shape, dt, kind="Internal").ap()

    xT_d = din("xT", [D, S])
    x_d = din("x", [S, D])
    memT_d = din("memT", [D, NMEM])
    posr_d = din("posr", [64, S], I32)
    w_in_d = din("w_in", [D, IN_COLS])
    w_mem_d = din("w_mem", [D, 1024])
    w_a_d = din("w_a", [512, D])
    w_b_d = din("w_b", [512, D])
    w_m_d = din("w_m", [512, D])
    w_out_d = din("w_out", [D, D])
    gpre_d = din("gpre", [128, KC])
    gmem_d = din("gmem", [128, KC])
    gpost_d = din("gpost", [128, D])
    bmerge_d = din("bmerge", [128, 24])
    bfg_d = din("bfg", [8, 1])
    c_ident_d = din("c_ident", [128, 128])
    c_mcur_d = din("c_mcur", [128, 128])
    c_mprev_d = din("c_mprev", [128, 128])
    c_perm_d = din("c_perm", [32, 32])
    c_inv_d = din("c_inv", [64, 2])
    y_d = nc.dram_tensor("y", [S, D], F32, kind="ExternalOutput").ap()

    wbf_d = dscr("wbf", [D, IN_COLS], BF16)
    wmembf_d = dscr("wmembf", [D, 1024], BF16)
    wabf_d = dscr("wabf", [512, D], BF16)
    wbbf_d = dscr("wbbf", [512, D], BF16)
    wmbf_d = dscr("wmbf", [512, D], BF16)
    woutbf_d = dscr("woutbf", [D, D], BF16)
    hT_d = dscr("hT", [D, S], BF16)
    cpos_d = dscr("cpos", [8, 3, S], BF16)
    cneg_d = dscr("cneg", [8, 3, S], BF16)
    yga_d = dscr("yga", [512, S], BF16)
    ygb_d = dscr("ygb", [512, S], BF16)
    ygm_d = dscr("ygm", [512, S], BF16)

    hT_v = hT_d.rearrange("(k p) s -> p k s", p=128)
    xT_v = xT_d.rearrange("(k p) s -> p k s", p=128)
    wbf_v = wbf_d.rearrange("(k p) c -> p k c", p=128)

    with ExitStack() as es:
        B = Bld(nc, es)
        pe, act, dve, pool, sp = nc.tensor, nc.scalar, nc.vector, nc.gpsimd, nc.sync

        def sb(stack, name, shape, dt):
            return stack.enter_context(nc.sbuf_tensor("sb_" + name, shape, dt))

        psA = [es.enter_context(nc.psum_tensor("psA%d" % i, [128, 1024], F32)) for i in range(4)]
        ps = []
        for i in range(4):
            ps.append(psA[i][:, 0:512])
            ps.append(psA[i][:, 512:1024])

        wsem = {}

        def cast_region(name, dst, src, rows, c0, c1):
            ds = wsem.get(name)
            if ds is None:
                ds = wsem[name] = B.dsem(track=False)
            c = c0
            while c < c1:
                ce = min(c + 2048, c1)
                for r0 in range(0, rows, 128):
                    B.dma("pool", dst[r0:r0 + 128, c:ce], src[r0:r0 + 128, c:ce], ds)
                c = ce

        def wtok(name):
            return (wsem[name].sem, wsem[name].n)

        ones_bf = sb(es, "ones_bf", [128, 128], BF16)
        ident_bf = sb(es, "ident_bf", [128, 128], BF16)
        mcur_bf = sb(es, "mcur_bf", [128, 128], BF16)
        mprev_bf = sb(es, "mprev_bf", [128, 128], BF16)
        perm_bf = sb(es, "perm_bf", [32, 32], BF16)
        gpre = sb(es, "gpre", [128, KC], F32)
        gmem = sb(es, "gmem", [128, KC], F32)
        gpost = sb(es, "gpost", [128, D], F32)
        bmerge = sb(es, "bmerge", [128, 24], F32)
        nbf = sb(es, "nbf", [8, 1], F32)
        cinv = sb(es, "cinv", [64, 2], F32)
        KmT = sb(es, "KmT", [128, 4, NMEM], BF16)
        Vm = sb(es, "Vm", [128, 2, 512], BF16)

        cds = B.dsem()
        cdp = B.dsem()
        B.dma("pool", ident_bf[:], c_ident_d[:, :], cdp)
        B.dma("pool", mcur_bf[:], c_mcur_d[:, :], cdp)
        B.dma("pool", mprev_bf[:], c_mprev_d[:, :], cdp)
        ctokp = B.dma("pool", perm_bf[:], c_perm_d[:, :], cdp)
        B.dma("sp", gpre[:], gpre_d[:, :], cds)
        B.dma("sp", gmem[:], gmem_d[:, :], cds)
        B.dma("sp", gpost[:], gpost_d[:, :], cds)
        B.dma("sp", bmerge[:], bmerge_d[:, :], cds)
        B.dma("sp", nbf[:], bfg_d[:, :], cds)
        ctok = B.dma("sp", cinv[:], c_inv_d[:, :], cds)

        cast_region("wmem", wmembf_d, w_mem_d, D, 0, 1024)
        cast_region("wfox", wbf_d, w_in_d, D, QB, QM)

        B.wait("dve", ctok, ctokp)
        t0 = B.sig("dve", dve.memset(ones_bf[:], 1.0))
        B.wait("dve", t0)
        t_nbf = B.sig("dve", dve.tensor_scalar(out=nbf[:], in0=nbf[:], scalar1=-1.0, scalar2=None, op0=ALU.mult))
        for e in ("pe", "act", "pool"):
            B.wait(e, ctok, ctokp, t0, t_nbf)

        with ExitStack() as p0:
            memt = sb(p0, "memt", [128, KC, NMEM], F32)
            msq = sb(p0, "msq", [128, KC, NMEM], BF16)
            hm = sb(p0, "hm", [128, KC, NMEM], BF16)
            mrt = sb(p0, "mrt", [128, NMEM], F32)
            mri = sb(p0, "mri", [128, NMEM], F32)
            wmem_sb = sb(p0, "wmem_sb", [128, KC, 1024], BF16)
            lds = B.dsem()
            lds_w = B.dsem()
            tl = B.dma("sp", memt[:], memT_d.rearrange("(k p) n -> p k n", p=128), lds)
            B.wait("sp", wtok("wmem"))
            tw = B.dma("sp", wmem_sb[:], wmembf_d.rearrange("(k p) c -> p k c", p=128), lds_w)
            B.wait("pool", tl)
            tq = B.sig("pool", pool.tensor_tensor(out=msq[:], in0=memt[:], in1=memt[:], op=ALU.mult))
            B.wait("pe", tq)
            for k in range(KC):
                mm = pe.matmul(ps[0][:, 0:NMEM], lhsT=ones_bf[:], rhs=msq[:, k, :], start=(k == 0), stop=(k == KC - 1))
            tp = B.sig("pe", mm)
            B.wait("dve", tp)
            t1 = B.sig("dve", dve.tensor_scalar(out=mri[:], in0=ps[0][:, 0:NMEM], scalar1=1.0 / D, scalar2=EPS, op0=ALU.mult, op1=ALU.add))
            B.wait("act", t1)
            ta = B.sig("act", act.activation(out=mrt[:], in_=mri[:], func=AF.Sqrt))
            B.wait("dve", ta)
            t2 = B.sig("dve", dve.reciprocal(out=mri[:], in_=mrt[:]))
            B.wait("dve", t2)
            for k in range(KC):
                th = B.sig("dve", dve.scalar_tensor_tensor(out=hm[:, k, :], in0=memt[:, k, :], scalar=gmem[:, k:k + 1], in1=mri[:], op0=ALU.mult, op1=ALU.mult))
            B.wait("pe", th, tw)
            for hh in range(4):
                for k in range(KC):
                    mm = pe.matmul(ps[1 + hh][:, 0:NMEM], lhsT=wmem_sb[:, k, hh * 128:(hh + 1) * 128], rhs=hm[:, k, :], start=(k == 0), stop=(k == KC - 1))
                tk = B.sig("pe", mm)
                B.wait("act", tk)
                B.sig("act", act.copy(out=KmT[:, hh, :], in_=ps[1 + hh][:, 0:NMEM]))
            for kt in range(2):
                for k in range(KC):
                    mm = pe.matmul(ps[5 + kt][:, :], lhsT=hm[:, k, kt * 128:(kt + 1) * 128], rhs=wmem_sb[:, k, 512:1024], start=(k == 0), stop=(k == KC - 1))
                tk = B.sig("pe", mm)
                B.wait("dve", tk)
                B.sig("dve", dve.tensor_copy(out=Vm[:, kt, :], in_=ps[5 + kt][:, :]))
            B.barrier()

        with ExitStack() as p0:
            FBt = sb(p0, "FBt", [8, S], F32)
            with ExitStack() as p0b:
                xt = [sb(p0b, "xt%d" % i, [128, KC, 512], F32) for i in range(2)]
                sq = [sb(p0b, "sq%d" % i, [128, KC, 512], BF16) for i in range(2)]
                hb = [sb(p0b, "hb%d" % i, [128, KC, 512], BF16) for i in range(2)]
                rt = [sb(p0b, "rt%d" % i, [128, 512], F32) for i in range(2)]
                ri = [sb(p0b, "ri%d" % i, [128, 512], F32) for i in range(2)]
                wfb = sb(p0b, "wfb", [128, KC, 8], BF16)
                lds = [B.dsem() for _ in range(2)]
                sds = [B.dsem() for _ in range(2)]
                wds = B.dsem()
                B.wait("sp", wtok("wfox"))
                twf = B.dma("sp", wfb[:], wbf_v[:, :, FBO:FBO + 8], wds)
                xt_free = [None, None]
                sq_free = [None, None]
                ssp_free = [None, None]
                rt_free = [None, None]
                ri_free = [None, None]
                hb_free = [[], []]
                fbp_free = [None, None]
                ltok = {}

                def load_x(blk):
                    s_ = blk % 2
                    B.wait("sp", xt_free[s_])
                    ltok[blk] = B.dma("sp", xt[s_][:], xT_v[:, :, blk * 512:(blk + 1) * 512], lds[s_])
                load_x(0)
                for blk in range(NB):
                    s = blk % 2
                    c0 = blk * 512
                    if blk + 1 < NB:
                        load_x(blk + 1)
                    tl = ltok[blk]
                    B.wait("pool", tl, sq_free[s])
                    tq = B.sig("pool", pool.tensor_tensor(out=sq[s][:], in0=xt[s][:], in1=xt[s][:], op=ALU.mult))
                    B.wait("pe", tq, ssp_free[s])
                    for k in range(KC):
                        mm = pe.matmul(ps[s][:, :], lhsT=ones_bf[:], rhs=sq[s][:, k, :], start=(k == 0), stop=(k == KC - 1))
                    tp = B.sig("pe", mm)
                    sq_free[s] = tp
                    B.wait("dve", tp, rt_free[s])
                    t1 = B.sig("dve", dve.tensor_scalar(out=rt[s][:], in0=ps[s][:, :], scalar1=1.0 / D, scalar2=EPS, op0=ALU.mult, op1=ALU.add))
                    ssp_free[s] = t1
                    B.wait("act", t1)
                    ta = B.sig("act", act.activation(out=rt[s][:], in_=rt[s][:], func=AF.Sqrt))
                    B.wait("dve", ta, ri_free[s])
                    t2 = B.sig("dve", dve.reciprocal(out=ri[s][:], in_=rt[s][:]))
                    rt_free[s] = t2
                    B.wait("dve", t2, tl, *hb_free[s])
                    for k in range(KC):
                        th = B.sig("dve", dve.scalar_tensor_tensor(out=hb[s][:, k, :], in0=xt[s][:, k, :], scalar=gpre[:, k:k + 1], in1=ri[s][:], op0=ALU.mult, op1=ALU.mult))
                    xt_free[s] = [th, tq]
                    ri_free[s] = th
                    B.wait("sp", th)
                    tst = B.dma("sp", hT_v[:, :, c0:c0 + 512], hb[s][:], sds[s])
                    B.wait("pe", th, twf, fbp_free[s])
                    for k in range(KC):
                        mm = pe.matmul(ps[2 + s][0:8, :], lhsT=wfb[:, k, :], rhs=hb[s][:, k, :], start=(k == 0), stop=(k == KC - 1))
                    tf = B.sig("pe", mm)
                    hb_free[s] = [tst, tf]
                    B.wait("dve", tf)
                    tc = B.sig("dve", dve.tensor_copy(out=FBt[:, c0:c0 + 512], in_=ps[2 + s][0:8, :]))
                    fbp_free[s] = tc
                B.barrier()
                cast_region("wA", wbf_d, w_in_d, D, 0, QB)
                cast_region("wM", wbf_d, w_in_d, D, QM, GL)
                cast_region("wG", wbf_d, w_in_d, D, GL, IN_COLS)
                cast_region("wa", wabf_d, w_a_d, 512, 0, D)
                cast_region("wb", wbbf_d, w_b_d, 512, 0, D)
                cast_region("wm", wmbf_d, w_m_d, 512, 0, D)
                cast_region("wout", woutbf_d, w_out_d, D, 0, D)
            with ExitStack() as p0c:
                Ct = sb(p0c, "Ct", [8, S], F32)
                P3 = sb(p0c, "P3", [8, 3, S], BF16)
                te = B.sig("act", act.activation(out=FBt[:], in_=FBt[:], func=AF.Exp, bias=nbf[:, 0:1], scale=-1.0))
                B.wait("act", te)
                tln = B.sig("act", act.activation(out=FBt[:], in_=FBt[:], func=AF.Ln, bias=1.0, scale=1.0))
                B.wait("dve", tln)
                tsc = B.sig("dve", dve.tensor_tensor_scan(out=Ct[:], data0=FBt[:], data1=FBt[:], initial=0.0, op0=ALU.add, op1=ALU.bypass))
                for i in range(3):
                    B.wait("dve", B.last("dve"))
                    tcp = B.sig("dve", dve.tensor_copy(out=P3[:, i, :], in_=Ct[:]))
                    if i < 2:
                        B.wait("dve", tcp)
                        B.sig("dve", dve.tensor_tensor(out=Ct[:], in0=Ct[:], in1=P3[:, i, :], op=ALU.subtract))
                B.wait("sp", tcp)
                cds2 = B.dsem()
                tcd = B.dma("sp", cpos_d[:, :, :], P3[:], cds2)
                B.wait("dve", tcd)
                tng = B.sig("dve", dve.tensor_scalar(out=P3[:], in0=P3[:], scalar1=-1.0, scalar2=None, op0=ALU.mult))
                B.wait("sp", tng)
                B.dma("sp", cneg_d[:, :, :], P3[:], cds2)
                B.barrier()

        with ExitStack() as p1:
            qT = [sb(p1, "qT%d" % i, [70, S], BF16) for i in range(2)]
            kT = [sb(p1, "kT%d" % i, [70, S], BF16) for i in range(2)]
            Va = [sb(p1, "Va%d" % i, [128, NT, 128], BF16) for i in range(2)]
            sz = [sb(p1, "sz%d" % i, [64, S], BF16) for i in range(2)]
            hs = [sb(p1, "hs%d" % i, [128, KC, 512], BF16) for i in range(2)]
            wq = sb(p1, "wq", [128, KC, 128], BF16)
            wk = sb(p1, "wk", [128, KC, 128], BF16)
            wv = sb(p1, "wv", [128, KC, 128], BF16)
            wz = sb(p1, "wz", [128, KC, 128], BF16)
            Pt = [sb(p1, "Pt%d" % i, [128, 1024], BF16) for i in range(3)]
            rec = [sb(p1, "rec%d" % i, [64, 512], F32) for i in range(2)]
            ytmp = [sb(p1, "ytmp%d" % i, [64, 512], F32) for i in range(2)]
            yo = [sb(p1, "yo%d" % i, [64, 512], BF16) for i in range(2)]
            hds = [B.dsem() for _ in range(2)]
            wds = B.dsem()
            ads = B.dsem()
            yds = [B.dsem() for _ in range(2)]
            for i in range(2):
                B.sig("dve", dve.memset(qT[i][64:70, :], 1.0))
                B.sig("dve", dve.memset(kT[i][64:70, :], 1.0))
                B.sig("dve", dve.memset(Va[i][:, :, 64:128], 1.0))
            tms = B.last("dve")
            B.wait("sp", tms)
            B.wait("pe", tms)
            att_done = None
            nrm_done = None
            hs_free = [None, None]
            psb_free = {}
            yds_tok = [None, None]
            for hp in range(4):
                B.wait("sp", att_done, nrm_done)
                c = hp * 128
                B.dma("sp", wq[:], wbf_v[:, :, QB + c:QB + c + 128], wds)
                B.dma("sp", wk[:], wbf_v[:, :, KB + c:KB + c + 128], wds)
                B.dma("sp", wv[:], wbf_v[:, :, VB + c:VB + c + 128], wds)
                tw = B.dma("sp", wz[:], wbf_v[:, :, ZB + c:ZB + c + 128], wds)
                for i in range(2):
                    h = 2 * hp + i
                    B.dma("sp", qT[i][64:67, :], cneg_d[h, :, :], ads)
                    ta_ = B.dma("sp", kT[i][67:70, :], cpos_d[h, :, :], ads)
                for e in ("act", "dve"):
                    B.wait(e, att_done, nrm_done)
                B.wait("pe", tw, ta_, B.last("act"))
                ltok = {}

                def load_h(blk):
                    s_ = blk % 2
                    B.wait("sp", hs_free[s_])
                    ltok[blk] = B.dma("sp", hs[s_][:], hT_v[:, :, blk * 512:(blk + 1) * 512], hds[s_])
                load_h(0)
                for blk in range(NB):
                    s = blk % 2
                    c0 = blk * 512
                    if blk + 1 < NB:
                        load_h(blk + 1)
                    B.wait("pe", ltok[blk])
                    for (wt, bank, kind) in ((wq, 0, "q"), (wk, 1, "k"), (wz, 2, "z")):
                        B.wait("pe", psb_free.get(bank))
                        for k in range(KC):
                            mm = pe.matmul(ps[bank][:, :], lhsT=wt[:, k, :], rhs=hs[s][:, k, :], start=(k == 0), stop=(k == KC - 1))
                        tp = B.sig("pe", mm)
                        if kind == "q":
                            B.wait("act", tp)
                            B.sig("act", act.mul(qT[0][0:64, c0:c0 + 512], ps[bank][0:64, :], 0.125))
                            psb_free[bank] = B.sig("act", act.mul(qT[1][0:64, c0:c0 + 512], ps[bank][64:128, :], 0.125))
                        elif kind == "k":
                            B.wait("dve", tp)
                            B.sig("dve", dve.tensor_copy(out=kT[0][0:64, c0:c0 + 512], in_=ps[bank][0:64, :]))
                            psb_free[bank] = B.sig("dve", dve.tensor_copy(out=kT[1][0:64, c0:c0 + 512], in_=ps[bank][64:128, :]))
                        else:
                            B.wait("act", tp)
                            B.sig("act", act.activation(out=sz[0][:, c0:c0 + 512], in_=ps[bank][0:64, :], func=AF.Silu))
                            psb_free[bank] = B.sig("act", act.activation(out=sz[1][:, c0:c0 + 512], in_=ps[bank][64:128, :], func=AF.Silu))
                    B.wait("pe", psb_free.get(3))
                    for tt in range(4):
                        for k in range(KC):
                            mm = pe.matmul(ps[3][:, tt * 128:(tt + 1) * 128], lhsT=hs[s][:, k, tt * 128:(tt + 1) * 128], rhs=wv[:, k, :], start=(k == 0), stop=(k == KC - 1))
                    tp = B.sig("pe", mm)
                    hs_free[s] = tp
                    B.wait("dve", tp)
                    pv = ps[3][:, :].rearrange("p (t c) -> p t c", c=128)
                    B.sig("dve", dve.tensor_copy(out=Va[0][:, blk * 4:blk * 4 + 4, 0:64], in_=pv[:, :, 0:64]))
                    psb_free[3] = B.sig("dve", dve.tensor_copy(out=Va[1][:, blk * 4:blk * 4 + 4, 0:64], in_=pv[:, :, 64:128]))
                proj_done = [B.last("act"), B.last("dve")]
                B.wait("pe", *proj_done)
                steps = []
                for i in range(2):
                    for qb in range(NB):
                        nj = 4 * qb + 4
                        for j in range(0, 4 * qb, 2):
                            steps.append((i, qb, [j, j + 1], nj))
                        for j in range(4 * qb, nj):
                            steps.append((i, qb, [j], nj))
                sbank_free = [None, None]
                pt_free = [None, None, None]
                acc_free = [None, None]
                exp_tok = [None] * len(steps)
                accn = [0]
                acc_of = {}

                def emit_qk(n):
                    i, qb, js, nj = steps[n]
                    sp_ = psA[n % 2]
                    B.wait("pe", sbank_free[n % 2])
                    for t_, j in enumerate(js):
                        m = j - 4 * qb
                        cc = 128 * m if m >= 0 else 0
                        o = 512 * t_
                        mm = pe.matmul(sp_[:, o + cc:o + 512], lhsT=kT[i][:, j * 128:(j + 1) * 128], rhs=qT[i][:, qb * 512 + cc:(qb + 1) * 512], start=True, stop=(m < 0))
                        if m >= 0:
                            mm = pe.matmul(sp_[:, o + cc:o + cc + 128], lhsT=ident_bf[:], rhs=mcur_bf[:], start=False, stop=True)
                    tp = B.sig("pe", mm)
                    lo = cc if len(js) == 1 else 0
                    hi = 512 * len(js)
                    B.wait("act", tp, pt_free[n % 3])
                    te_ = B.sig("act", act.activation(out=Pt[n % 3][:, lo:hi], in_=sp_[:, lo:hi], func=AF.Exp))
                    sbank_free[n % 2] = te_
                    exp_tok[n] = te_

                def emit_pv(n):
                    i, qb, js, nj = steps[n]
                    if js[0] == 0:
                        acc_of[(i, qb)] = accn[0] % 2
                        accn[0] += 1
                        B.wait("pe", acc_free[acc_of[(i, qb)]])
                    a = acc_of[(i, qb)]
                    B.wait("pe", exp_tok[n])
                    for t_, j in enumerate(js):
                        m = j - 4 * qb
                        cc = 128 * m if m >= 0 else 0
                        o = 512 * t_
                        mm = pe.matmul(ps[6 + a][:, cc:512], lhsT=Va[i][:, j, :], rhs=Pt[n % 3][:, o + cc:o + 512], start=(j == 0), stop=(j == nj - 1))
                    tp = B.sig("pe", mm)
                    pt_free[n % 3] = tp
                    if js[-1] == nj - 1:
                        h = 2 * hp + i
                        B.wait("dve", tp, yds_tok[a])
                        t1 = B.sig("dve", dve.reciprocal(out=rec[a][:], in_=ps[6 + a][64:128, :]))
                        B.wait("dve", t1)
                        t2 = B.sig("dve", dve.tensor_tensor(out=ytmp[a][:], in0=ps[6 + a][0:64, :], in1=rec[a][:], op=ALU.mult))
                        acc_free[a] = t2
                        B.wait("dve", t2)
                        t3 = B.sig("dve", dve.tensor_tensor(out=yo[a][:], in0=ytmp[a][:], in1=sz[i][:, qb * 512:(qb + 1) * 512], op=ALU.mult))
                        B.wait("sp", t3)
                        yds_tok[a] = B.dma("sp", ygb_d[h * 64:(h + 1) * 64, qb * 512:(qb + 1) * 512], yo[a][:], yds[a])

                for n in range(len(steps) + 1):
                    if n < len(steps):
                        emit_qk(n)
                    if n >= 1:
                        emit_pv(n - 1)
                att_done = B.last("pe")
                nrm_done = B.last("dve")
            B.barrier()

        with ExitStack() as pmid:
            cosT = sb(pmid, "cosT", [32, S], BF16)
            sinT = sb(pmid, "sinT", [32, S], BF16)
            with ExitStack() as p0:
                CW = min(2048, S)
                posi = sb(p0, "posi", [64, CW], I32)
                ang = sb(p0, "ang", [64, CW], F32)
                tmpf = sb(p0, "tmpf", [64, CW], F32)
                tmpi = sb(p0, "tmpi", [64, CW], I32)
                lds = B.dsem()
                prev = None
                for c0 in range(0, S, CW):
                    B.wait("sp", prev)
                    tl = B.dma("sp", posi[:], posr_d[:, c0:c0 + CW], lds)
                    B.wait("dve", tl, prev)

                    def dv(inst):
                        t = B.sig("dve", inst)
                        B.wait("dve", t)
                        return t
                    dv(dve.tensor_copy(out=ang[:], in_=posi[:]))
                    dv(dve.tensor_scalar(out=ang[:], in0=ang[:], scalar1=cinv[:, 0:1], scalar2=cinv[:, 1:2], op0=ALU.mult, op1=ALU.add))
                    dv(dve.tensor_scalar(out=tmpf[:], in0=ang[:], scalar1=1.0 / TWO_PI, scalar2=None, op0=ALU.mult))
                    dv(dve.tensor_copy(out=tmpi[:], in_=tmpf[:]))
                    dv(dve.tensor_copy(out=tmpf[:], in_=tmpi[:]))
                    dv(dve.scalar_tensor_tensor(out=ang[:], in0=tmpf[:], scalar=-TWO_PI, in1=ang[:], op0=ALU.mult, op1=ALU.add))
                    dv(dve.tensor_scalar(out=tmpf[:], in0=ang[:], scalar1=PI, scalar2=-TWO_PI, op0=ALU.is_gt, op1=ALU.mult))
                    dv(dve.tensor_tensor(out=ang[:], in0=ang[:], in1=tmpf[:], op=ALU.add))
                    dv(dve.tensor_scalar(out=tmpf[:], in0=ang[:], scalar1=-PI, scalar2=TWO_PI, op0=ALU.is_lt, op1=ALU.mult))
                    dv(dve.tensor_tensor(out=ang[:], in0=ang[:], in1=tmpf[:], op=ALU.add))
                    td = dv(dve.tensor_scalar(out=ang[:], in0=ang[:], scalar1=PI, scalar2=-PI, op0=ALU.min, op1=ALU.max))
                    B.wait("act", td)
                    B.sig("act", act.activation(out=sinT[:, c0:c0 + CW], in_=ang[0:32, :], func=AF.Sin))
                    prev = B.sig("act", act.activation(out=cosT[:, c0:c0 + CW], in_=ang[32:64, :], func=AF.Sin))
                B.barrier()

            with ExitStack() as p2:
                qT2 = sb(p2, "qT2", [128, S], BF16)
                kT2 = sb(p2, "kT2", [128, S], BF16)
                vT2 = sb(p2, "vT2", [128, S], BF16)
                NDA = sb(p2, "NDA", [128, 2, S], BF16)
                hs = [sb(p2, "hs2_%d" % i, [128, KC, 512], BF16) for i in range(2)]
                w6 = [sb(p2, "w6_%d" % i, [128, KC, 128], BF16) for i in range(3)]
                Vc = [sb(p2, "Vc%d" % i, [128, 128], BF16) for i in range(6)]
                P2 = [sb(p2, "P2_%d" % i, [128, 256], BF16) for i in range(4)]
                rt1 = [sb(p2, "rt1_%d" % i, [32, 512], F32) for i in range(2)]
                rt2 = [sb(p2, "rt2_%d" % i, [32, 512], F32) for i in range(2)]
                sza = [sb(p2, "sza%d" % i, [128, 512], BF16) for i in range(2)]
                szm = [sb(p2, "szm%d" % i, [128, 512], BF16) for i in range(2)]
                qm = [sb(p2, "qm%d" % i, [128, 512], BF16) for i in range(2)]
                Pm = [sb(p2, "Pm%d" % i, [128, 2, 512], BF16) for i in range(2)]
                recm = [sb(p2, "recm%d" % i, [128, 512], F32) for i in range(2)]
                tmpm = [sb(p2, "tmpm%d" % i, [128, 512], F32) for i in range(2)]
                yoa = [sb(p2, "yoa%d" % i, [128, 512], BF16) for i in range(2)]
                yom = [sb(p2, "yom%d" % i, [128, 512], BF16) for i in range(2)]
                hds = [B.dsem() for _ in range(2)]
                wds = B.dsem()
                yads = [B.dsem() for _ in range(2)]
                ymds = [B.dsem() for _ in range(2)]
                SC = float(128.0 ** -0.5)
                B.wait("sp", wtok("wA"), wtok("wM"))
                hs_free = [None, None]
                sweep_done = None
                ya_tok = [None, None]
                ym_tok = [None, None]
                for hh in range(4):
                    for sweep in range(4):
                        B.wait("sp", sweep_done)
                        for e in ("pe", "act", "dve", "pool"):
                            B.wait(e, sweep_done)
                        if sweep < 3:
                            Hc = (sweep * 4 + hh) * 128
                            cols = [QA + Hc, KA + Hc, VA + Hc]
                        else:
                            cols = [ZA + hh * 128, QM + hh * 128, ZM + hh * 128]
                        for wi, cc in enumerate(cols):
                            tw = B.dma("sp", w6[wi][:], wbf_v[:, :, cc:cc + 128], wds)
                        B.wait("pe", tw)
                        bank_free = {}
                        ltok = {}

                        def load_h2(blk):
                            s_ = blk % 2
                            B.wait("sp", hs_free[s_])
                            ltok[blk] = B.dma("sp", hs[s_][:], hT_v[:, :, blk * 512:(blk + 1) * 512], hds[s_])
                        load_h2(0)
                        for blk in range(NB):
                            s = blk % 2
                            c0 = blk * 512
                            if blk + 1 < NB:
                                load_h2(blk + 1)
                            B.wait("pe", ltok[blk])
                            tps = []
                            for wi in range(3):
                                B.wait("pe", bank_free.get(wi))
                                for k in range(KC):
                                    mm = pe.matmul(ps[wi][:, :], lhsT=w6[wi][:, k, :], rhs=hs[s][:, k, :], start=(k == 0), stop=(k == KC - 1))
                                tps.append(B.sig("pe", mm))
                            hs_free[s] = tps[-1]
                            if sweep < 3:
                                B.wait("act", tps[0])
                                tq = B.sig("act", act.copy(out=qT2[:, c0:c0 + 512], in_=ps[0][:, :]))
                                bank_free[0] = tq
                                B.wait("dve", tps[1])
                                tk_ = B.sig("dve", dve.tensor_copy(out=kT2[:, c0:c0 + 512], in_=ps[1][:, :]))
                                bank_free[1] = tk_
                                B.wait("act", tps[2])
                                bank_free[2] = B.sig("act", act.copy(out=vT2[:, c0:c0 + 512], in_=ps[2][:, :]))
                                for ri_, (tt_, traw) in enumerate(((qT2, tq), (kT2, tk_))):
                                    bk = 3 + ri_
                                    B.wait("pe", traw, bank_free.get(bk))
                                    tsw = B.sig("pe", pe.matmul(ps[bk][0:32, :], lhsT=perm_bf[:], rhs=tt_[0:32, c0:c0 + 512], start=True, stop=True))
                                    B.wait("pool", traw, bank_free.get(("rt1", ri_)))
                                    tp1 = B.sig("pool", pool.tensor_tensor(out=rt1[ri_][:], in0=tt_[0:32, c0:c0 + 512], in1=cosT[:, c0:c0 + 512], op=ALU.mult))
                                    B.wait("dve", tsw, bank_free.get(("rt2", ri_)))
                                    tp2 = B.sig("dve", dve.tensor_tensor(out=rt2[ri_][:], in0=ps[bk][0:32, :], in1=sinT[:, c0:c0 + 512], op=ALU.mult))
                                    bank_free[bk] = tp2
                                    B.wait("pool", tp1, tp2, tsw)
                                    tp3 = B.sig("pool", pool.tensor_tensor(out=tt_[0:32, c0:c0 + 512], in0=rt1[ri_][:], in1=rt2[ri_][:], op=ALU.add))
                                    bank_free[("rt1", ri_)] = tp3
                                    bank_free[("rt2", ri_)] = tp3
                            else:
                                a = blk % 2
                                B.wait("act", tps[0], bank_free.get(("sza", a)))
                                tz = B.sig("act", act.activation(out=sza[a][:], in_=ps[0][:, :], func=AF.Silu))
                                bank_free[0] = tz
                                B.wait("dve", bank_free.get(("recm", a)))
                                t1 = B.sig("dve", dve.reciprocal(out=recm[a][:], in_=NDA[:, 1, c0:c0 + 512]))
                                B.wait("dve", t1)
                                t2 = B.sig("dve", dve.tensor_tensor(out=tmpm[a][:], in0=NDA[:, 0, c0:c0 + 512], in1=recm[a][:], op=ALU.mult))
                                B.wait("dve", t2, tz, ya_tok[a])
                                t3 = B.sig("dve", dve.tensor_tensor(out=yoa[a][:], in0=tmpm[a][:], in1=sza[a][:], op=ALU.mult))
                                bank_free[("sza", a)] = t3
                                B.wait("sp", t3)
                                ya_tok[a] = B.dma("sp", yga_d[hh * 128:(hh + 1) * 128, c0:c0 + 512], yoa[a][:], yads[a])
                                B.wait("act", tps[1], bank_free.get(("qm", a)))
                                tqm = B.sig("act", act.copy(out=qm[a][:], in_=ps[1][:, :]))
                                bank_free[1] = tqm
                                B.wait("act", tps[2], bank_free.get(("szm", a)))
                                tzm = B.sig("act", act.activation(out=szm[a][:], in_=ps[2][:, :], func=AF.Silu))
                                bank_free[2] = tzm
                                B.wait("pe", tqm)
                                for kt in range(2):
                                    B.wait("pe", bank_free.get(3 + kt))
                                    tsm = B.sig("pe", pe.matmul(ps[3 + kt][:, :], lhsT=KmT[:, hh, kt * 128:(kt + 1) * 128], rhs=qm[a][:], start=True, stop=True))
                                    B.wait("act", tsm, bank_free.get(("Pm", a)))
                                    bank_free[3 + kt] = B.sig("act", act.activation(out=Pm[a][:, kt, :], in_=ps[3 + kt][:, :], func=AF.Exp, scale=SC))
                                texp = B.last("act")
                                bank_free[("qm", a)] = tsm
                                B.wait("pe", texp, bank_free.get(5), bank_free.get(6))
                                for kt in range(2):
                                    mm = pe.matmul(ps[5][:, :], lhsT=Vm[:, kt, hh * 128:(hh + 1) * 128], rhs=Pm[a][:, kt, :], start=(kt == 0), stop=(kt == 1))
                                for kt in range(2):
                                    mm = pe.matmul(ps[6][:, :], lhsT=ones_bf[:], rhs=Pm[a][:, kt, :], start=(kt == 0), stop=(kt == 1))
                                tpv = B.sig("pe", mm)
                                bank_free[("Pm", a)] = tpv
                                B.wait("dve", tpv, t3)
                                t1 = B.sig("dve", dve.reciprocal(out=recm[a][:], in_=ps[6][:, :]))
                                bank_free[6] = t1
                                B.wait("dve", t1)
                                t2 = B.sig("dve", dve.tensor_tensor(out=tmpm[a][:], in0=ps[5][:, :], in1=recm[a][:], op=ALU.mult))
                                bank_free[5] = t2
                                B.wait("dve", t2, tzm, ym_tok[a])
                                t3 = B.sig("dve", dve.tensor_tensor(out=yom[a][:], in0=tmpm[a][:], in1=szm[a][:], op=ALU.mult))
                                bank_free[("szm", a)] = t3
                                bank_free[("recm", a)] = t3
                                B.wait("sp", t3)
                                ym_tok[a] = B.dma("sp", ygm_d[hh * 128:(hh + 1) * 128, c0:c0 + 512], yom[a][:], ymds[a])
                        if sweep < 3:
                            dl = DILS[sweep]
                            nbk = S // (128 * dl)
                            rot_done = [B.last("pool"), B.last("act"), B.last("dve")]
                            B.wait("pe", *rot_done)
                            blocks = [(r, n) for r in range(dl) for n in range(nbk)]
                            NBK_ = len(blocks)
                            NU, NV, SK = 4, 6, 2
                            s2_free = [None] * NU
                            p2_free = [None] * NU
                            nd_free = [None] * NU
                            pv_tok = [None] * NBK_
                            tvc_tok = [None] * NBK_
                            exp_tok2 = [None] * NBK_

                            def cls_(r, nn):
                                st = r + 128 * nn * dl
                                return slice(st, st + 127 * dl + 1, dl)

                            def stage_a(i):
                                r, n = blocks[i]
                                u = i % NU
                                vs = i % NV
                                tpb = ps[4 + u][:, 256:512].bitcast(BF16)[:, 0:128]
                                s2 = ps[u][:, 0:256]
                                B.wait("pe", nd_free[u])
                                ttp = B.sig("pe", pe.transpose(out=tpb, in_=vT2[:, cls_(r, n)], identity=ident_bf[:]))
                                lastreader = i - NV + 1
                                B.wait("dve", ttp, pv_tok[lastreader] if lastreader >= 0 else None)
                                tvc_tok[i] = B.sig("dve", dve.tensor_copy(out=Vc[vs][:], in_=tpb))
                                W = 256 if n > 0 else 128
                                B.wait("pe", s2_free[u])
                                pe.matmul(s2[:, 0:128], lhsT=kT2[:, cls_(r, n)], rhs=qT2[:, cls_(r, n)], start=True, stop=False)
                                mm = pe.matmul(s2[:, 0:128], lhsT=ident_bf[:], rhs=mcur_bf[:], start=False, stop=True)
                                if n > 0:
                                    pe.matmul(s2[:, 128:256], lhsT=kT2[:, cls_(r, n - 1)], rhs=qT2[:, cls_(r, n)], start=True, stop=False)
                                    mm = pe.matmul(s2[:, 128:256], lhsT=ident_bf[:], rhs=mprev_bf[:], start=False, stop=True)
                                ts2 = B.sig("pe", mm)
                                B.wait("act", ts2, p2_free[u])
                                tex = B.sig("act", act.activation(out=P2[u][:, 0:W], in_=s2[:, 0:W], func=AF.Exp, scale=SC))
                                s2_free[u] = tex
                                exp_tok2[i] = tex

                            def stage_b(i):
                                r, n = blocks[i]
                                u = i % NU
                                vs = i % NV
                                vp = (i - 1) % NV
                                nd = ps[4 + u]
                                B.wait("pe", exp_tok2[i], tvc_tok[i])
                                pe.matmul(nd[:, 0:128], lhsT=Vc[vs][:], rhs=P2[u][:, 0:128], start=True, stop=(n == 0))
                                if n > 0:
                                    pe.matmul(nd[:, 0:128], lhsT=Vc[vp][:], rhs=P2[u][:, 128:256], start=False, stop=True)
                                mm = pe.matmul(nd[:, 128:256], lhsT=ones_bf[:], rhs=P2[u][:, 0:128], start=True, stop=(n == 0))
                                if n > 0:
                                    mm = pe.matmul(nd[:, 128:256], lhsT=ones_bf[:], rhs=P2[u][:, 128:256], start=False, stop=True)
                                tnd = B.sig("pe", mm)
                                p2_free[u] = tnd
                                pv_tok[i] = tnd
                                B.wait("dve", tnd)
                                ndv = nd[:, 0:256].rearrange("p (a c) -> p a c", a=2)
                                if sweep == 0:
                                    tac = B.sig("dve", dve.tensor_copy(out=NDA[:, :, cls_(r, n)], in_=ndv))
                                else:
                                    tac = B.sig("dve", dve.tensor_tensor(out=NDA[:, :, cls_(r, n)], in0=ndv, in1=NDA[:, :, cls_(r, n)], op=ALU.add))
                                nd_free[u] = tac

                            for it in range(NBK_ + SK):
                                if it < NBK_:
                                    stage_a(it)
                                if it >= SK:
                                    stage_b(it - SK)
                        sweep_done = [B.last("pe"), B.last("act"), B.last("dve"), B.last("pool")]
                B.barrier()

        with ExitStack() as p3:
            Wgl = sb(p3, "Wgl", [128, KC, 3072], BF16)
            Wbr = [sb(p3, "Wbr%d" % i, [128, 4, D], BF16) for i in range(3)]
            Wo = sb(p3, "Wo", [128, KC, D], BF16)
            hs = [sb(p3, "hs3_%d" % i, [128, KC, 512], BF16) for i in range(2)]
            yg = [sb(p3, "yg%d" % b_, [128, 4, 512], BF16) for b_ in range(3)]
            G = sb(p3, "G", [128, 24, 512], BF16)
            mg = sb(p3, "mg", [128, KC, 512], BF16)
            m1 = sb(p3, "m1", [128, 512], F32)
            m2 = sb(p3, "m2", [128, 512], F32)
            m3 = sb(p3, "m3", [128, 512], F32)
            xtok = [sb(p3, "xtok%d" % i, [128, D], F32) for i in range(2)]
            ytok = [sb(p3, "ytok%d" % i, [128, D], F32) for i in range(2)]
            junk = sb(p3, "junk", [128, 512], BF16)
            ssq = [sb(p3, "ssq%d" % i, [128, 4], F32) for i in range(2)]
            lds = [B.dsem() for _ in range(2)]
            ygds = B.dsem()
            xds = [B.dsem() for _ in range(2)]
            ods = [B.dsem() for _ in range(2)]
            wds = B.dsem()
            B.wait("sp", wtok("wG"), wtok("wa"), wtok("wb"), wtok("wm"), wtok("wout"))
            for k in range(KC):
                B.dma("sp", Wgl[:, k, :], wbf_d[k * 128:(k + 1) * 128, GL:GL + 3072], wds)
            for b_, wd in enumerate((wabf_d, wbbf_d, wmbf_d)):
                B.dma("sp", Wbr[b_][:], wd.rearrange("(e p) c -> p e c", p=128), wds)
            tw = B.dma("sp", Wo[:], woutbf_d.rearrange("(k p) c -> p k c", p=128), wds)
            B.wait("pe", tw)
            ygd = (yga_d, ygb_d, ygm_d)
            hs_free = [None, None]
            yg_free = None
            G_free = None
            mg_free = None
            bank_free = {}
            xt_free = [None, None]
            yt_free = [None, None]
            tcnt = 0
            ltok = {}

            def load_h3(blk):
                s_ = blk % 2
                B.wait("sp", hs_free[s_])
                ltok[blk] = B.dma("sp", hs[s_][:], hT_v[:, :, blk * 512:(blk + 1) * 512], lds[s_])
            load_h3(0)
            xtk = {}

            def load_xt(ti):
                a_ = ti % 2
                B.wait("sp", xt_free[a_])
                xtk[ti] = B.dma("sp", xtok[a_][:], x_d[ti * 128:(ti + 1) * 128, :], xds[a_])
            load_xt(0)
            st3 = {"yg_free": None, "G_free": None, "mg_free": None, "tyg": None, "tmg": None}

            def load_yg(blk):
                c0 = blk * 512
                B.wait("sp", st3["yg_free"])
                for b_ in range(3):
                    st3["tyg"] = B.dma("sp", yg[b_][:], ygd[b_].rearrange("(e p) s -> p e s", p=128)[:, :, c0:c0 + 512], ygds)

            def gates(blk, lo, hi):
                s = blk % 2
                if lo == 0:
                    B.wait("pe", ltok[blk])
                    B.wait("act", st3["G_free"])
                for gi in range(lo, hi):
                    bk = gi % 2
                    B.wait("pe", bank_free.get(bk))
                    for k in range(KC):
                        mm = pe.matmul(ps[bk][:, :], lhsT=Wgl[:, k, gi * 128:(gi + 1) * 128], rhs=hs[s][:, k, :], start=(k == 0), stop=(k == KC - 1))
                    tp = B.sig("pe", mm)
                    B.wait("act", tp)
                    bank_free[bk] = B.sig("act", act.activation(out=G[:, gi, :], in_=ps[bk][:, :], func=AF.Sigmoid, bias=bmerge[:, gi:gi + 1]))
                if hi == 24:
                    st3["tg"] = B.last("act")
                    hs_free[s] = B.last("pe")

            def branch(blk):
                B.wait("dve", st3["tg"])
                B.wait("pool", st3["mg_free"])
                B.wait("pe", st3["tyg"])
                for c in range(KC):
                    tb = []
                    for b_ in range(3):
                        bk = 2 + b_
                        B.wait("pe", bank_free.get(bk))
                        for e_ in range(4):
                            mm = pe.matmul(ps[bk][:, :], lhsT=Wbr[b_][:, e_, c * 128:(c + 1) * 128], rhs=yg[b_][:, e_, :], start=(e_ == 0), stop=(e_ == 3))
                        tb.append(B.sig("pe", mm))
                    mt = (m1, m2, m3)
                    tms_ = []
                    for b_ in range(3):
                        B.wait("dve", tb[b_], bank_free.get("m"))
                        t_ = B.sig("dve", dve.tensor_tensor(out=mt[b_][:], in0=ps[2 + b_][:, :], in1=G[:, b_ * 8 + c, :], op=ALU.mult))
                        bank_free[2 + b_] = t_
                        tms_.append(t_)
                    B.wait("pool", *tms_)
                    ta1 = B.sig("pool", pool.tensor_tensor(out=m1[:], in0=m1[:], in1=m2[:], op=ALU.add))
                    B.wait("pool", ta1)
                    ta2 = B.sig("pool", pool.tensor_tensor(out=mg[:, c, :], in0=m1[:], in1=m3[:], op=ALU.add))
                    bank_free["m"] = ta2
                st3["G_free"] = B.last("dve")
                st3["yg_free"] = B.last("pe")
                st3["tmg"] = B.last("pool")

            def outproj(blk, tt):
                ti = blk * 4 + tt
                a = ti % 2
                r0 = ti * 128
                if ti + 1 < NT:
                    load_xt(ti + 1)
                tx = xtk[ti]
                B.wait("pe", st3["tmg"])
                for half in range(2):
                    bk = 5 + half
                    B.wait("pe", bank_free.get(bk))
                    for k in range(KC):
                        mm = pe.matmul(ps[bk][:, :], lhsT=mg[:, k, tt * 128:(tt + 1) * 128], rhs=Wo[:, k, half * 512:(half + 1) * 512], start=(k == 0), stop=(k == KC - 1))
                to = B.sig("pe", mm)
                if tt == 3:
                    st3["mg_free"] = to
                B.wait("act", to, bank_free.get(("ssq", a)))
                for half in range(2):
                    tsq = B.sig("act", act.activation(out=junk[:], in_=ps[5 + half][:, :], func=AF.Square, accum_out=ssq[a][:, half:half + 1]))
                B.wait("dve", tsq)
                t1 = B.sig("dve", dve.tensor_tensor(out=ssq[a][:, 2:3], in0=ssq[a][:, 0:1], in1=ssq[a][:, 1:2], op=ALU.add))
                B.wait("dve", t1)
                t2 = B.sig("dve", dve.tensor_scalar(out=ssq[a][:, 2:3], in0=ssq[a][:, 2:3], scalar1=1.0 / D, scalar2=EPS, op0=ALU.mult, op1=ALU.add))
                B.wait("act", t2)
                t3 = B.sig("act", act.activation(out=ssq[a][:, 3:4], in_=ssq[a][:, 2:3], func=AF.Sqrt))
                B.wait("dve", t3)
                t4 = B.sig("dve", dve.reciprocal(out=ssq[a][:, 3:4], in_=ssq[a][:, 3:4]))
                B.wait("dve", t4, yt_free[a])
                for half in range(2):
                    t5 = B.sig("dve", dve.scalar_tensor_tensor(out=ytok[a][:, half * 512:(half + 1) * 512], in0=ps[5 + half][:, :], scalar=ssq[a][:, 3:4], in1=gpost[:, half * 512:(half + 1) * 512], op0=ALU.mult, op1=ALU.mult))
                    bank_free[5 + half] = t5
                bank_free[("ssq", a)] = t5
                B.wait("pool", t5, tx)
                t6 = B.sig("pool", pool.tensor_tensor(out=ytok[a][:], in0=ytok[a][:], in1=xtok[a][:], op=ALU.add))
                xt_free[a] = t6
                B.wait("sp", t6)
                yt_free[a] = B.dma("sp", y_d[r0:r0 + 128, :], ytok[a][:], ods[a])

            load_yg(0)
            if NB > 1:
                load_h3(1)
            gates(0, 0, 24)
            branch(0)
            for blk in range(NB):
                nxt = blk + 1 < NB
                if nxt:
                    load_yg(blk + 1)
                    if blk + 2 < NB:
                        pass
                for tt in range(4):
                    if nxt:
                        if tt == 0 and blk + 2 < NB:
                            pass
                        gates(blk + 1, 6 * tt, 6 * tt + 6)
                        if tt == 3 and blk + 2 < NB:
                            load_h3(blk + 2)
                    outproj(blk, tt)
                if nxt:
                    branch(blk + 1)
            B.barrier()
    return nc


def _consts():
    ident = np.eye(128, dtype=np.float32)
    p = np.arange(128)[:, None]
    f = np.arange(128)[None, :]
    mcur = np.where(p <= f, 0.0, NEGM).astype(np.float32)
    mprev = np.where(p >= f, 0.0, NEGM).astype(np.float32)
    perm = np.zeros((32, 32), np.float32)
    for m in range(16):
        perm[m + 16, m] = -1.0
        perm[m, m + 16] = 1.0
    half = 16
    inv = (np.float32(500000.0) ** (-np.arange(half, dtype=np.float32) / np.float32(half))).astype(np.float32)
    cinv = np.zeros((64, 2), np.float32)
    for q in range(64):
        cinv[q, 0] = inv[q % 16]
        cinv[q, 1] = 0.0 if q < 32 else np.float32(np.pi / 2)
    return ident, mcur, mprev, perm, cinv


def make_in_map(b, S, x, mem, positions, norm_pre_g, norm_post_g, norm_mem_g, w_in, b_forget, b_merge,
                w_mem_kv, w_branch_a, w_branch_b, w_branch_m, w_out):
    ident, mcur, mprev, perm, cinv = _consts()
    f32 = np.float32
    return {
        "xT": np.ascontiguousarray(x[b].T, dtype=f32),
        "x": np.ascontiguousarray(x[b], dtype=f32),
        "memT": np.ascontiguousarray(mem[b].T, dtype=f32),
        "posr": np.ascontiguousarray(np.broadcast_to(positions[b][None, :], (64, S)), dtype=np.int32),
        "w_in": np.ascontiguousarray(w_in[0], dtype=f32),
        "w_mem": np.ascontiguousarray(w_mem_kv[0], dtype=f32),
        "w_a": np.ascontiguousarray(w_branch_a[0], dtype=f32),
        "w_b": np.ascontiguousarray(w_branch_b[0], dtype=f32),
        "w_m": np.ascontiguousarray(w_branch_m[0], dtype=f32),
        "w_out": np.ascontiguousarray(w_out[0], dtype=f32),
        "gpre": np.ascontiguousarray(norm_pre_g[0].reshape(KC, 128).T, dtype=f32),
        "gmem": np.ascontiguousarray(norm_mem_g[0].reshape(KC, 128).T, dtype=f32),
        "gpost": np.ascontiguousarray(np.broadcast_to(norm_post_g[0][None, :], (128, D)), dtype=f32),
        "bmerge": np.ascontiguousarray(b_merge[0].reshape(24, 128).T, dtype=f32),
        "bfg": np.ascontiguousarray(b_forget[0].reshape(8, 1), dtype=f32),
        "c_ident": ident, "c_mcur": mcur, "c_mprev": mprev, "c_perm": perm, "c_inv": cinv,
    }


def kernel(**inputs):
    inputs = {k: np.asarray(v) for k, v in inputs.items()}
    x = inputs["x"]
    Bsz, S, _ = x.shape
    nc = build_program(S)
    in_maps = [make_in_map(b, S, **inputs) for b in range(Bsz)]
    res = run_bass_kernel_spmd(nc, in_maps, core_ids=list(range(Bsz)))
    out = np.stack([np.asarray(r["y"], dtype=np.float32) for r in res.results], axis=0)
    return out
```

```python
import numpy as np
from contextlib import ExitStack
import concourse.bass as bass
import concourse.mybir as mybir
from concourse.bass_utils import run_bass_kernel_spmd

F32 = mybir.dt.float32
BF16 = mybir.dt.bfloat16
I32 = mybir.dt.int32
AF = mybir.ActivationFunctionType
ALU = mybir.AluOpType

D = 1024
KC = 8
NMEM = 256
IN_COLS = 11272
QA, KA, VA, ZA = 0, 1536, 3072, 4608
QB, KB, VB, FBO, ZB = 5120, 5632, 6144, 6656, 6664
QM, ZM, GL = 7176, 7688, 8200
EPS = 1e-6
NEGM = -30000.0
TWO_PI = float(2.0 * np.pi)
PI = float(np.pi)
DILS = (1, 4, 16)


class DSem:
    def __init__(self, sem):
        self.sem = sem
        self.n = 0


class Bld:
    def __init__(self, nc, es):
        self.nc = nc
        self.es = es
        self.eng = {"pe": nc.tensor, "act": nc.scalar, "dve": nc.vector, "pool": nc.gpsimd, "sp": nc.sync}
        self.sem = {e: es.enter_context(nc.semaphore("s_" + e)) for e in ("pe", "act", "dve", "pool")}
        self.cnt = {e: 0 for e in self.sem}
        self.seen = {e: {} for e in self.eng}
        self.dsems = []
        self.nds = 0

    def sig(self, e, inst):
        inst.then_inc(self.sem[e], 1)
        self.cnt[e] += 1
        return (self.sem[e], self.cnt[e])

    def last(self, e):
        return (self.sem[e], self.cnt[e])

    def wait(self, e, *toks):
        for t in toks:
            if t is None:
                continue
            if isinstance(t, list):
                self.wait(e, *t)
                continue
            sem, v = t
            if v <= 0:
                continue
            k = id(sem)
            if self.seen[e].get(k, 0) >= v:
                continue
            self.eng[e].wait_ge(sem, v)
            self.seen[e][k] = v

    def dsem(self, track=True):
        s = self.es.enter_context(self.nc.semaphore("d%d" % self.nds))
        self.nds += 1
        d = DSem(s)
        if track:
            self.dsems.append(d)
        return d

    def dma(self, q, out, in_, ds):
        inst = self.eng[q].dma_start(out=out, in_=in_)
        ds.n += 16
        inst.then_inc(ds.sem, 16)
        return (ds.sem, ds.n)

    def barrier(self):
        toks = [self.last(e) for e in self.sem] + [(d.sem, d.n) for d in self.dsems]
        for e in self.eng:
            self.wait(e, *toks)


def build_program(S):
    NB = S // 512
    NT = S // 128
    nc = bass.Bass("TRN2", target_bir_lowering=False)

    def din(name, shape, dt=F32):
        return nc.dram_tensor(name, shape, dt, kind="ExternalInput").ap()

    def dscr(name, shape, dt):
        return nc.dram_tensor(name, shape, dt, kind="Internal").ap()

    xT_d = din("xT", [D, S])
    x_d = din("x", [S, D])
    memT_d = din("memT", [D, NMEM])
    posr_d = din("posr", [64, S], I32)
    w_in_d = din("w_in", [D, IN_COLS])
    w_mem_d = din("w_mem", [D, 1024])
    w_a_d = din("w_a", [512, D])
    w_b_d = din("w_b", [512, D])
    w_m_d = din("w_m", [512, D])
    w_out_d = din("w_out", [D, D])
    gpre_d = din("gpre", [128, KC])
    gmem_d = din("gmem", [128, KC])
    gpost_d = din("gpost", [128, D])
    bmerge_d = din("bmerge", [128, 24])
    bfg_d = din("bfg", [8, 1])
    c_ident_d = din("c_ident", [128, 128])
    c_mcur_d = din("c_mcur", [128, 128])
    c_mprev_d = din("c_mprev", [128, 128])
    c_perm_d = din("c_perm", [32, 32])
    c_inv_d = din("c_inv", [64, 2])
    y_d = nc.dram_tensor("y", [S, D], F32, kind="ExternalOutput").ap()

    wbf_d = dscr("wbf", [D, IN_COLS], BF16)
    wmembf_d = dscr("wmembf", [D, 1024], BF16)
    wabf_d = dscr("wabf", [512, D], BF16)
    wbbf_d = dscr("wbbf", [512, D], BF16)
    wmbf_d = dscr("wmbf", [512, D], BF16)
    woutbf_d = dscr("woutbf", [D, D], BF16)
    hT_d = dscr("hT", [D, S], BF16)
    cpos_d = dscr("cpos", [8, 3, S], BF16)
    cneg_d = dscr("cneg", [8, 3, S], BF16)
    yga_d = dscr("yga", [512, S], BF16)
    ygb_d = dscr("ygb", [512, S], BF16)
    ygm_d = dscr("ygm", [512, S], BF16)

    hT_v = hT_d.rearrange("(k p) s -> p k s", p=128)
    xT_v = xT_d.rearrange("(k p) s -> p k s", p=128)
    wbf_v = wbf_d.rearrange("(k p) c -> p k c", p=128)

    with ExitStack() as es:
        B = Bld(nc, es)
        pe, act, dve, pool, sp = nc.tensor, nc.scalar, nc.vector, nc.gpsimd, nc.sync

        def sb(stack, name, shape, dt):
            return stack.enter_context(nc.sbuf_tensor("sb_" + name, shape, dt))

        psS = [es.enter_context(nc.psum_tensor("psS%d" % i, [128, 1536], F32)) for i in range(2)]
        psC = es.enter_context(nc.psum_tensor("psC", [128, 1024], F32))
        ps = []
        for i in range(2):
            for j_ in range(3):
                ps.append(psS[i][:, 512 * j_:512 * (j_ + 1)])
        ps.append(psC[:, 0:512])
        ps.append(psC[:, 512:1024])

        wsem = {}

        def cast_region(name, dst, src, rows, c0, c1):
            ds = wsem.get(name)
            if ds is None:
                ds = wsem[name] = B.dsem(track=False)
            c = c0
            while c < c1:
                ce = min(c + 2048, c1)
                for r0 in range(0, rows, 128):
                    B.dma("pool", dst[r0:r0 + 128, c:ce], src[r0:r0 + 128, c:ce], ds)
                c = ce

        def wtok(name):
            return (wsem[name].sem, wsem[name].n)

        ones_bf = sb(es, "ones_bf", [128, 128], BF16)
        ident_bf = sb(es, "ident_bf", [128, 128], BF16)
        mcur_bf = sb(es, "mcur_bf", [128, 128], BF16)
        mprev_bf = sb(es, "mprev_bf", [128, 128], BF16)
        perm_bf = sb(es, "perm_bf", [32, 32], BF16)
        gpre = sb(es, "gpre", [128, KC], F32)
        gmem = sb(es, "gmem", [128, KC], F32)
        gpost = sb(es, "gpost", [128, D], F32)
        bmerge = sb(es, "bmerge", [128, 24], F32)
        nbf = sb(es, "nbf", [8, 1], F32)
        cinv = sb(es, "cinv", [64, 2], F32)
        KmT = sb(es, "KmT", [128, 4, NMEM], BF16)
        Vm = sb(es, "Vm", [128, 2, 512], BF16)

        cds = B.dsem()
        cdp = B.dsem()
        B.dma("pool", ident_bf[:], c_ident_d[:, :], cdp)
        B.dma("pool", mcur_bf[:], c_mcur_d[:, :], cdp)
        B.dma("pool", mprev_bf[:], c_mprev_d[:, :], cdp)
        ctokp = B.dma("pool", perm_bf[:], c_perm_d[:, :], cdp)
        B.dma("sp", gpre[:], gpre_d[:, :], cds)
        B.dma("sp", gmem[:], gmem_d[:, :], cds)
        B.dma("sp", gpost[:], gpost_d[:, :], cds)
        B.dma("sp", bmerge[:], bmerge_d[:, :], cds)
        B.dma("sp", nbf[:], bfg_d[:, :], cds)
        ctok = B.dma("sp", cinv[:], c_inv_d[:, :], cds)

        cast_region("wmem", wmembf_d, w_mem_d, D, 0, 1024)
        cast_region("wfox", wbf_d, w_in_d, D, QB, QM)

        B.wait("dve", ctok, ctokp)
        t0 = B.sig("dve", dve.memset(ones_bf[:], 1.0))
        B.wait("dve", t0)
        t_nbf = B.sig("dve", dve.tensor_scalar(out=nbf[:], in0=nbf[:], scalar1=-1.0, scalar2=None, op0=ALU.mult))
        for e in ("pe", "act", "pool"):
            B.wait(e, ctok, ctokp, t0, t_nbf)

        with ExitStack() as p0:
            memt = sb(p0, "memt", [128, KC, NMEM], F32)
            msq = sb(p0, "msq", [128, KC, NMEM], BF16)
            hm = sb(p0, "hm", [128, KC, NMEM], BF16)
            mrt = sb(p0, "mrt", [128, NMEM], F32)
            mri = sb(p0, "mri", [128, NMEM], F32)
            wmem_sb = sb(p0, "wmem_sb", [128, KC, 1024], BF16)
            lds = B.dsem()
            lds_w = B.dsem()
            tl = B.dma("sp", memt[:], memT_d.rearrange("(k p) n -> p k n", p=128), lds)
            B.wait("sp", wtok("wmem"))
            tw = B.dma("sp", wmem_sb[:], wmembf_d.rearrange("(k p) c -> p k c", p=128), lds_w)
            B.wait("pool", tl)
            tq = B.sig("pool", pool.tensor_tensor(out=msq[:], in0=memt[:], in1=memt[:], op=ALU.mult))
            B.wait("pe", tq)
            for k in range(KC):
                mm = pe.matmul(ps[0][:, 0:NMEM], lhsT=ones_bf[:], rhs=msq[:, k, :], start=(k == 0), stop=(k == KC - 1))
            tp = B.sig("pe", mm)
            B.wait("dve", tp)
            t1 = B.sig("dve", dve.tensor_scalar(out=mri[:], in0=ps[0][:, 0:NMEM], scalar1=1.0 / D, scalar2=EPS, op0=ALU.mult, op1=ALU.add))
            B.wait("act", t1)
            ta = B.sig("act", act.activation(out=mrt[:], in_=mri[:], func=AF.Sqrt))
            B.wait("dve", ta)
            t2 = B.sig("dve", dve.reciprocal(out=mri[:], in_=mrt[:]))
            B.wait("dve", t2)
            for k in range(KC):
                th = B.sig("dve", dve.scalar_tensor_tensor(out=hm[:, k, :], in0=memt[:, k, :], scalar=gmem[:, k:k + 1], in1=mri[:], op0=ALU.mult, op1=ALU.mult))
            B.wait("pe", th, tw)
            for hh in range(4):
                for k in range(KC):
                    mm = pe.matmul(ps[1 + hh][:, 0:NMEM], lhsT=wmem_sb[:, k, hh * 128:(hh + 1) * 128], rhs=hm[:, k, :], start=(k == 0), stop=(k == KC - 1))
                tk = B.sig("pe", mm)
                B.wait("act", tk)
                B.sig("act", act.copy(out=KmT[:, hh, :], in_=ps[1 + hh][:, 0:NMEM]))
            for kt in range(2):
                for k in range(KC):
                    mm = pe.matmul(ps[5 + kt][:, :], lhsT=hm[:, k, kt * 128:(kt + 1) * 128], rhs=wmem_sb[:, k, 512:1024], start=(k == 0), stop=(k == KC - 1))
                tk = B.sig("pe", mm)
                B.wait("dve", tk)
                B.sig("dve", dve.tensor_copy(out=Vm[:, kt, :], in_=ps[5 + kt][:, :]))
            B.barrier()

        with ExitStack() as p0:
            FBt = sb(p0, "FBt", [8, S], F32)
            with ExitStack() as p0b:
                xt = [sb(p0b, "xt%d" % i, [128, KC, 512], F32) for i in range(2)]
                sq = [sb(p0b, "sq%d" % i, [128, KC, 512], BF16) for i in range(2)]
                hb = [sb(p0b, "hb%d" % i, [128, KC, 512], BF16) for i in range(2)]
                rt = [sb(p0b, "rt%d" % i, [128, 512], F32) for i in range(2)]
                ri = [sb(p0b, "ri%d" % i, [128, 512], F32) for i in range(2)]
                wfb = sb(p0b, "wfb", [128, KC, 8], BF16)
                lds = [B.dsem() for _ in range(2)]
                sds = [B.dsem() for _ in range(2)]
                wds = B.dsem()
                B.wait("sp", wtok("wfox"))
                twf = B.dma("sp", wfb[:], wbf_v[:, :, FBO:FBO + 8], wds)
                xt_free = [None, None]
                sq_free = [None, None]
                ssp_free = [None, None]
                rt_free = [None, None]
                ri_free = [None, None]
                hb_free = [[], []]
                fbp_free = [None, None]
                ltok = {}

                def load_x(blk):
                    s_ = blk % 2
                    B.wait("sp", xt_free[s_])
                    ltok[blk] = B.dma("sp", xt[s_][:], xT_v[:, :, blk * 512:(blk + 1) * 512], lds[s_])
                load_x(0)
                for blk in range(NB):
                    s = blk % 2
                    c0 = blk * 512
                    if blk + 1 < NB:
                        load_x(blk + 1)
                    tl = ltok[blk]
                    B.wait("pool", tl, sq_free[s])
                    tq = B.sig("pool", pool.tensor_tensor(out=sq[s][:], in0=xt[s][:], in1=xt[s][:], op=ALU.mult))
                    B.wait("pe", tq, ssp_free[s])
                    for k in range(KC):
                        mm = pe.matmul(ps[s][:, :], lhsT=ones_bf[:], rhs=sq[s][:, k, :], start=(k == 0), stop=(k == KC - 1))
                    tp = B.sig("pe", mm)
                    sq_free[s] = tp
                    B.wait("dve", tp, rt_free[s])
                    t1 = B.sig("dve", dve.tensor_scalar(out=rt[s][:], in0=ps[s][:, :], scalar1=1.0 / D, scalar2=EPS, op0=ALU.mult, op1=ALU.add))
                    ssp_free[s] = t1
                    B.wait("act", t1)
                    ta = B.sig("act", act.activation(out=rt[s][:], in_=rt[s][:], func=AF.Sqrt))
                    B.wait("dve", ta, ri_free[s])
                    t2 = B.sig("dve", dve.reciprocal(out=ri[s][:], in_=rt[s][:]))
                    rt_free[s] = t2
                    B.wait("dve", t2, tl, *hb_free[s])
                    for k in range(KC):
                        th = B.sig("dve", dve.scalar_tensor_tensor(out=hb[s][:, k, :], in0=xt[s][:, k, :], scalar=gpre[:, k:k + 1], in1=ri[s][:], op0=ALU.mult, op1=ALU.mult))
                    xt_free[s] = [th, tq]
                    ri_free[s] = th
                    B.wait("sp", th)
                    tst = B.dma("sp", hT_v[:, :, c0:c0 + 512], hb[s][:], sds[s])
                    B.wait("pe", th, twf, fbp_free[s])
                    for k in range(KC):
                        mm = pe.matmul(ps[2 + s][0:8, :], lhsT=wfb[:, k, :], rhs=hb[s][:, k, :], start=(k == 0), stop=(k == KC - 1))
                    tf = B.sig("pe", mm)
                    hb_free[s] = [tst, tf]
                    B.wait("dve", tf)
                    tc = B.sig("dve", dve.tensor_copy(out=FBt[:, c0:c0 + 512], in_=ps[2 + s][0:8, :]))
                    fbp_free[s] = tc
                B.barrier()
                cast_region("wA", wbf_d, w_in_d, D, 0, QB)
                cast_region("wM", wbf_d, w_in_d, D, QM, GL)
                cast_region("wG", wbf_d, w_in_d, D, GL, IN_COLS)
                cast_region("wa", wabf_d, w_a_d, 512, 0, D)
                cast_region("wb", wbbf_d, w_b_d, 512, 0, D)
                cast_region("wm", wmbf_d, w_m_d, 512, 0, D)
                cast_region("wout", woutbf_d, w_out_d, D, 0, D)
            with ExitStack() as p0c:
                Ct = sb(p0c, "Ct", [8, S], F32)
                P3 = sb(p0c, "P3", [8, 3, S], BF16)
                te = B.sig("act", act.activation(out=FBt[:], in_=FBt[:], func=AF.Exp, bias=nbf[:, 0:1], scale=-1.0))
                B.wait("act", te)
                tln = B.sig("act", act.activation(out=FBt[:], in_=FBt[:], func=AF.Ln, bias=1.0, scale=1.0))
                B.wait("dve", tln)
                tsc = B.sig("dve", dve.tensor_tensor_scan(out=Ct[:], data0=FBt[:], data1=FBt[:], initial=0.0, op0=ALU.add, op1=ALU.bypass))
                for i in range(3):
                    B.wait("dve", B.last("dve"))
                    tcp = B.sig("dve", dve.tensor_copy(out=P3[:, i, :], in_=Ct[:]))
                    if i < 2:
                        B.wait("dve", tcp)
                        B.sig("dve", dve.tensor_tensor(out=Ct[:], in0=Ct[:], in1=P3[:, i, :], op=ALU.subtract))
                B.wait("sp", tcp)
                cds2 = B.dsem()
                tcd = B.dma("sp", cpos_d[:, :, :], P3[:], cds2)
                B.wait("dve", tcd)
                tng = B.sig("dve", dve.tensor_scalar(out=P3[:], in0=P3[:], scalar1=-1.0, scalar2=None, op0=ALU.mult))
                B.wait("sp", tng)
                B.dma("sp", cneg_d[:, :, :], P3[:], cds2)
                B.barrier()

        with ExitStack() as p1:
            qT = [sb(p1, "qT%d" % i, [70, S], BF16) for i in range(2)]
            kT = [sb(p1, "kT%d" % i, [70, S], BF16) for i in range(2)]
            Va = [sb(p1, "Va%d" % i, [128, NT, 128], BF16) for i in range(2)]
            sz = [sb(p1, "sz%d" % i, [64, S], BF16) for i in range(2)]
            hs = [sb(p1, "hs%d" % i, [128, KC, 512], BF16) for i in range(2)]
            wq = sb(p1, "wq", [128, KC, 128], BF16)
            wk = sb(p1, "wk", [128, KC, 128], BF16)
            wv = sb(p1, "wv", [128, KC, 128], BF16)
            wz = sb(p1, "wz", [128, KC, 128], BF16)
            Pt = [sb(p1, "Pt%d" % i, [128, 1536], BF16) for i in range(3)]
            rec = [sb(p1, "rec%d" % i, [64, 512], F32) for i in range(2)]
            ytmp = [sb(p1, "ytmp%d" % i, [64, 512], F32) for i in range(2)]
            yo = [sb(p1, "yo%d" % i, [64, 512], BF16) for i in range(2)]
            hds = [B.dsem() for _ in range(2)]
            wds = B.dsem()
            ads = B.dsem()
            yds = [B.dsem() for _ in range(2)]
            for i in range(2):
                B.sig("dve", dve.memset(qT[i][64:70, :], 1.0))
                B.sig("dve", dve.memset(kT[i][64:70, :], 1.0))
                B.sig("dve", dve.memset(Va[i][:, :, 64:128], 1.0))
            tms = B.last("dve")
            B.wait("sp", tms)
            B.wait("pe", tms)
            att_done = None
            nrm_done = None
            hs_free = [None, None]
            psb_free = {}
            yds_tok = [None, None]
            for hp in range(4):
                B.wait("sp", att_done, nrm_done)
                c = hp * 128
                B.dma("sp", wq[:], wbf_v[:, :, QB + c:QB + c + 128], wds)
                B.dma("sp", wk[:], wbf_v[:, :, KB + c:KB + c + 128], wds)
                B.dma("sp", wv[:], wbf_v[:, :, VB + c:VB + c + 128], wds)
                tw = B.dma("sp", wz[:], wbf_v[:, :, ZB + c:ZB + c + 128], wds)
                for i in range(2):
                    h = 2 * hp + i
                    B.dma("sp", qT[i][64:67, :], cneg_d[h, :, :], ads)
                    ta_ = B.dma("sp", kT[i][67:70, :], cpos_d[h, :, :], ads)
                for e in ("act", "dve"):
                    B.wait(e, att_done, nrm_done)
                B.wait("pe", tw, ta_, B.last("act"))
                ltok = {}

                def load_h(blk):
                    s_ = blk % 2
                    B.wait("sp", hs_free[s_])
                    ltok[blk] = B.dma("sp", hs[s_][:], hT_v[:, :, blk * 512:(blk + 1) * 512], hds[s_])
                load_h(0)
                for blk in range(NB):
                    s = blk % 2
                    c0 = blk * 512
                    if blk + 1 < NB:
                        load_h(blk + 1)
                    B.wait("pe", ltok[blk])
                    for (wt, bank, kind) in ((wq, 0, "q"), (wk, 1, "k"), (wz, 2, "z")):
                        B.wait("pe", psb_free.get(bank))
                        for k in range(KC):
                            mm = pe.matmul(ps[bank][:, :], lhsT=wt[:, k, :], rhs=hs[s][:, k, :], start=(k == 0), stop=(k == KC - 1))
                        tp = B.sig("pe", mm)
                        if kind == "q":
                            B.wait("act", tp)
                            B.sig("act", act.mul(qT[0][0:64, c0:c0 + 512], ps[bank][0:64, :], 0.125))
                            psb_free[bank] = B.sig("act", act.mul(qT[1][0:64, c0:c0 + 512], ps[bank][64:128, :], 0.125))
                        elif kind == "k":
                            B.wait("dve", tp)
                            B.sig("dve", dve.tensor_copy(out=kT[0][0:64, c0:c0 + 512], in_=ps[bank][0:64, :]))
                            psb_free[bank] = B.sig("dve", dve.tensor_copy(out=kT[1][0:64, c0:c0 + 512], in_=ps[bank][64:128, :]))
                        else:
                            B.wait("act", tp)
                            B.sig("act", act.activation(out=sz[0][:, c0:c0 + 512], in_=ps[bank][0:64, :], func=AF.Silu))
                            psb_free[bank] = B.sig("act", act.activation(out=sz[1][:, c0:c0 + 512], in_=ps[bank][64:128, :], func=AF.Silu))
                    B.wait("pe", psb_free.get(3))
                    for tt in range(4):
                        for k in range(KC):
                            mm = pe.matmul(ps[3][:, tt * 128:(tt + 1) * 128], lhsT=hs[s][:, k, tt * 128:(tt + 1) * 128], rhs=wv[:, k, :], start=(k == 0), stop=(k == KC - 1))
                    tp = B.sig("pe", mm)
                    hs_free[s] = tp
                    B.wait("dve", tp)
                    pv = ps[3][:, :].rearrange("p (t c) -> p t c", c=128)
                    B.sig("dve", dve.tensor_copy(out=Va[0][:, blk * 4:blk * 4 + 4, 0:64], in_=pv[:, :, 0:64]))
                    psb_free[3] = B.sig("dve", dve.tensor_copy(out=Va[1][:, blk * 4:blk * 4 + 4, 0:64], in_=pv[:, :, 64:128]))
                proj_done = [B.last("act"), B.last("dve")]
                B.wait("pe", *proj_done)
                steps = []
                for i in range(2):
                    for qb in range(NB):
                        nj = 4 * qb + 4
                        for j in range(0, 4 * qb, 3):
                            steps.append((i, qb, list(range(j, min(j + 3, 4 * qb))), nj))
                        for j in range(4 * qb, nj):
                            steps.append((i, qb, [j], nj))
                sbank_free = [None, None]
                pt_free = [None, None, None]
                acc_free = [None, None]
                exp_tok = [None] * len(steps)
                accn = [0]
                acc_of = {}

                def emit_qk(n):
                    i, qb, js, nj = steps[n]
                    sp_ = psS[n % 2]
                    B.wait("pe", sbank_free[n % 2])
                    for t_, j in enumerate(js):
                        m = j - 4 * qb
                        cc = 128 * m if m >= 0 else 0
                        o = 512 * t_
                        mm = pe.matmul(sp_[:, o + cc:o + 512], lhsT=kT[i][:, j * 128:(j + 1) * 128], rhs=qT[i][:, qb * 512 + cc:(qb + 1) * 512], start=True, stop=(m < 0))
                        if m >= 0:
                            mm = pe.matmul(sp_[:, o + cc:o + cc + 128], lhsT=ident_bf[:], rhs=mcur_bf[:], start=False, stop=True)
                    tp = B.sig("pe", mm)
                    lo = cc if len(js) == 1 else 0
                    hi = 512 * len(js)
                    B.wait("act", tp, pt_free[n % 3])
                    te_ = B.sig("act", act.activation(out=Pt[n % 3][:, lo:hi], in_=sp_[:, lo:hi], func=AF.Exp))
                    sbank_free[n % 2] = te_
                    exp_tok[n] = te_

                def emit_pv(n):
                    i, qb, js, nj = steps[n]
                    if js[0] == 0:
                        acc_of[(i, qb)] = accn[0] % 2
                        accn[0] += 1
                        B.wait("pe", acc_free[acc_of[(i, qb)]])
                    a = acc_of[(i, qb)]
                    B.wait("pe", exp_tok[n])
                    for t_, j in enumerate(js):
                        m = j - 4 * qb
                        cc = 128 * m if m >= 0 else 0
                        o = 512 * t_
                        mm = pe.matmul(ps[6 + a][:, cc:512], lhsT=Va[i][:, j, :], rhs=Pt[n % 3][:, o + cc:o + 512], start=(j == 0), stop=(j == nj - 1))
                    tp = B.sig("pe", mm)
                    pt_free[n % 3] = tp
                    if js[-1] == nj - 1:
                        h = 2 * hp + i
                        B.wait("dve", tp, yds_tok[a])
                        t1 = B.sig("dve", dve.reciprocal(out=rec[a][:], in_=ps[6 + a][64:128, :]))
                        B.wait("dve", t1)
                        t2 = B.sig("dve", dve.tensor_tensor(out=ytmp[a][:], in0=ps[6 + a][0:64, :], in1=rec[a][:], op=ALU.mult))
                        acc_free[a] = t2
                        B.wait("dve", t2)
                        t3 = B.sig("dve", dve.tensor_tensor(out=yo[a][:], in0=ytmp[a][:], in1=sz[i][:, qb * 512:(qb + 1) * 512], op=ALU.mult))
                        B.wait("sp", t3)
                        yds_tok[a] = B.dma("sp", ygb_d[h * 64:(h + 1) * 64, qb * 512:(qb + 1) * 512], yo[a][:], yds[a])

                for n in range(len(steps) + 1):
                    if n < len(steps):
                        emit_qk(n)
                    if n >= 1:
                        emit_pv(n - 1)
                att_done = B.last("pe")
                nrm_done = B.last("dve")
            B.barrier()

        with ExitStack() as pmid:
            cosT = sb(pmid, "cosT", [32, S], BF16)
            sinT = sb(pmid, "sinT", [32, S], BF16)
            with ExitStack() as p0:
                CW = min(2048, S)
                posi = sb(p0, "posi", [64, CW], I32)
                ang = sb(p0, "ang", [64, CW], F32)
                tmpf = sb(p0, "tmpf", [64, CW], F32)
                tmpi = sb(p0, "tmpi", [64, CW], I32)
                lds = B.dsem()
                prev = None
                for c0 in range(0, S, CW):
                    B.wait("sp", prev)
                    tl = B.dma("sp", posi[:], posr_d[:, c0:c0 + CW], lds)
                    B.wait("dve", tl, prev)

                    def dv(inst):
                        t = B.sig("dve", inst)
                        B.wait("dve", t)
                        return t
                    dv(dve.tensor_copy(out=ang[:], in_=posi[:]))
                    dv(dve.tensor_scalar(out=ang[:], in0=ang[:], scalar1=cinv[:, 0:1], scalar2=cinv[:, 1:2], op0=ALU.mult, op1=ALU.add))
                    dv(dve.tensor_scalar(out=tmpf[:], in0=ang[:], scalar1=1.0 / TWO_PI, scalar2=None, op0=ALU.mult))
                    dv(dve.tensor_copy(out=tmpi[:], in_=tmpf[:]))
                    dv(dve.tensor_copy(out=tmpf[:], in_=tmpi[:]))
                    dv(dve.scalar_tensor_tensor(out=ang[:], in0=tmpf[:], scalar=-TWO_PI, in1=ang[:], op0=ALU.mult, op1=ALU.add))
                    dv(dve.tensor_scalar(out=tmpf[:], in0=ang[:], scalar1=PI, scalar2=-TWO_PI, op0=ALU.is_gt, op1=ALU.mult))
                    dv(dve.tensor_tensor(out=ang[:], in0=ang[:], in1=tmpf[:], op=ALU.add))
                    dv(dve.tensor_scalar(out=tmpf[:], in0=ang[:], scalar1=-PI, scalar2=TWO_PI, op0=ALU.is_lt, op1=ALU.mult))
                    dv(dve.tensor_tensor(out=ang[:], in0=ang[:], in1=tmpf[:], op=ALU.add))
                    td = dv(dve.tensor_scalar(out=ang[:], in0=ang[:], scalar1=PI, scalar2=-PI, op0=ALU.min, op1=ALU.max))
                    B.wait("act", td)
                    B.sig("act", act.activation(out=sinT[:, c0:c0 + CW], in_=ang[0:32, :], func=AF.Sin))
                    prev = B.sig("act", act.activation(out=cosT[:, c0:c0 + CW], in_=ang[32:64, :], func=AF.Sin))
                B.barrier()

            with ExitStack() as p2:
                qT2 = sb(p2, "qT2", [128, S], BF16)
                kT2 = sb(p2, "kT2", [128, S], BF16)
                vT2 = sb(p2, "vT2", [128, S], BF16)
                NDA = sb(p2, "NDA", [128, 2, S], BF16)
                hs = [sb(p2, "hs2_%d" % i, [128, KC, 512], BF16) for i in range(2)]
                w6 = [sb(p2, "w6_%d" % i, [128, KC, 128], BF16) for i in range(3)]
                Vc = [sb(p2, "Vc%d" % i, [128, 128], BF16) for i in range(6)]
                P2 = [sb(p2, "P2_%d" % i, [128, 256], BF16) for i in range(4)]
                rt1 = [sb(p2, "rt1_%d" % i, [32, 512], F32) for i in range(2)]
                rt2 = [sb(p2, "rt2_%d" % i, [32, 512], F32) for i in range(2)]
                sza = [sb(p2, "sza%d" % i, [128, 512], BF16) for i in range(2)]
                szm = [sb(p2, "szm%d" % i, [128, 512], BF16) for i in range(2)]
                qm = [sb(p2, "qm%d" % i, [128, 512], BF16) for i in range(2)]
                Pm = [sb(p2, "Pm%d" % i, [128, 2, 512], BF16) for i in range(2)]
                recm = [sb(p2, "recm%d" % i, [128, 512], F32) for i in range(2)]
                tmpm = [sb(p2, "tmpm%d" % i, [128, 512], F32) for i in range(2)]
                yoa = [sb(p2, "yoa%d" % i, [128, 512], BF16) for i in range(2)]
                yom = [sb(p2, "yom%d" % i, [128, 512], BF16) for i in range(2)]
                hds = [B.dsem() for _ in range(2)]
                wds = B.dsem()
                yads = [B.dsem() for _ in range(2)]
                ymds = [B.dsem() for _ in range(2)]
                SC = float(128.0 ** -0.5)
                B.wait("sp", wtok("wA"), wtok("wM"))
                hs_free = [None, None]
                sweep_done = None
                ya_tok = [None, None]
                ym_tok = [None, None]
                for hh in range(4):
                    for sweep in range(4):
                        B.wait("sp", sweep_done)
                        for e in ("pe", "act", "dve", "pool"):
                            B.wait(e, sweep_done)
                        if sweep < 3:
                            Hc = (sweep * 4 + hh) * 128
                            cols = [QA + Hc, KA + Hc, VA + Hc]
                        else:
                            cols = [ZA + hh * 128, QM + hh * 128, ZM + hh * 128]
                        for wi, cc in enumerate(cols):
                            tw = B.dma("sp", w6[wi][:], wbf_v[:, :, cc:cc + 128], wds)
                        B.wait("pe", tw)
                        bank_free = {}
                        ltok = {}

                        def load_h2(blk):
                            s_ = blk % 2
                            B.wait("sp", hs_free[s_])
                            ltok[blk] = B.dma("sp", hs[s_][:], hT_v[:, :, blk * 512:(blk + 1) * 512], hds[s_])
                        load_h2(0)
                        for blk in range(NB):
                            s = blk % 2
                            c0 = blk * 512
                            if blk + 1 < NB:
                                load_h2(blk + 1)
                            B.wait("pe", ltok[blk])
                            tps = []
                            for wi in range(3):
                                B.wait("pe", bank_free.get(wi))
                                for k in range(KC):
                                    mm = pe.matmul(ps[wi][:, :], lhsT=w6[wi][:, k, :], rhs=hs[s][:, k, :], start=(k == 0), stop=(k == KC - 1))
                                tps.append(B.sig("pe", mm))
                            hs_free[s] = tps[-1]
                            if sweep < 3:
                                B.wait("act", tps[0])
                                tq = B.sig("act", act.copy(out=qT2[:, c0:c0 + 512], in_=ps[0][:, :]))
                                bank_free[0] = tq
                                B.wait("dve", tps[1])
                                tk_ = B.sig("dve", dve.tensor_copy(out=kT2[:, c0:c0 + 512], in_=ps[1][:, :]))
                                bank_free[1] = tk_
                                B.wait("act", tps[2])
                                bank_free[2] = B.sig("act", act.copy(out=vT2[:, c0:c0 + 512], in_=ps[2][:, :]))
                                for ri_, (tt_, traw) in enumerate(((qT2, tq), (kT2, tk_))):
                                    bk = 3 + ri_
                                    B.wait("pe", traw, bank_free.get(bk))
                                    tsw = B.sig("pe", pe.matmul(ps[bk][0:32, :], lhsT=perm_bf[:], rhs=tt_[0:32, c0:c0 + 512], start=True, stop=True))
                                    B.wait("pool", traw, bank_free.get(("rt1", ri_)))
                                    tp1 = B.sig("pool", pool.tensor_tensor(out=rt1[ri_][:], in0=tt_[0:32, c0:c0 + 512], in1=cosT[:, c0:c0 + 512], op=ALU.mult))
                                    B.wait("dve", tsw, bank_free.get(("rt2", ri_)))
                                    tp2 = B.sig("dve", dve.tensor_tensor(out=rt2[ri_][:], in0=ps[bk][0:32, :], in1=sinT[:, c0:c0 + 512], op=ALU.mult))
                                    bank_free[bk] = tp2
                                    B.wait("pool", tp1, tp2, tsw)
                                    tp3 = B.sig("pool", pool.tensor_tensor(out=tt_[0:32, c0:c0 + 512], in0=rt1[ri_][:], in1=rt2[ri_][:], op=ALU.add))
                                    bank_free[("rt1", ri_)] = tp3
                                    bank_free[("rt2", ri_)] = tp3
                            else:
                                a = blk % 2
                                B.wait("act", tps[0], bank_free.get(("sza", a)))
                                tz = B.sig("act", act.activation(out=sza[a][:], in_=ps[0][:, :], func=AF.Silu))
                                bank_free[0] = tz
                                B.wait("dve", bank_free.get(("recm", a)))
                                t1 = B.sig("dve", dve.reciprocal(out=recm[a][:], in_=NDA[:, 1, c0:c0 + 512]))
                                B.wait("dve", t1)
                                t2 = B.sig("dve", dve.tensor_tensor(out=tmpm[a][:], in0=NDA[:, 0, c0:c0 + 512], in1=recm[a][:], op=ALU.mult))
                                B.wait("dve", t2, tz, ya_tok[a])
                                t3 = B.sig("dve", dve.tensor_tensor(out=yoa[a][:], in0=tmpm[a][:], in1=sza[a][:], op=ALU.mult))
                                bank_free[("sza", a)] = t3
                                B.wait("sp", t3)
                                ya_tok[a] = B.dma("sp", yga_d[hh * 128:(hh + 1) * 128, c0:c0 + 512], yoa[a][:], yads[a])
                                B.wait("act", tps[1], bank_free.get(("qm", a)))
                                tqm = B.sig("act", act.copy(out=qm[a][:], in_=ps[1][:, :]))
                                bank_free[1] = tqm
                                B.wait("act", tps[2], bank_free.get(("szm", a)))
                                tzm = B.sig("act", act.activation(out=szm[a][:], in_=ps[2][:, :], func=AF.Silu))
                                bank_free[2] = tzm
                                B.wait("pe", tqm)
                                for kt in range(2):
                                    B.wait("pe", bank_free.get(3 + kt))
                                    tsm = B.sig("pe", pe.matmul(ps[3 + kt][:, :], lhsT=KmT[:, hh, kt * 128:(kt + 1) * 128], rhs=qm[a][:], start=True, stop=True))
                                    B.wait("act", tsm, bank_free.get(("Pm", a)))
                                    bank_free[3 + kt] = B.sig("act", act.activation(out=Pm[a][:, kt, :], in_=ps[3 + kt][:, :], func=AF.Exp, scale=SC))
                                texp = B.last("act")
                                bank_free[("qm", a)] = tsm
                                B.wait("pe", texp, bank_free.get(5), bank_free.get(6))
                                for kt in range(2):
                                    mm = pe.matmul(ps[5][:, :], lhsT=Vm[:, kt, hh * 128:(hh + 1) * 128], rhs=Pm[a][:, kt, :], start=(kt == 0), stop=(kt == 1))
                                for kt in range(2):
                                    mm = pe.matmul(ps[6][:, :], lhsT=ones_bf[:], rhs=Pm[a][:, kt, :], start=(kt == 0), stop=(kt == 1))
                                tpv = B.sig("pe", mm)
                                bank_free[("Pm", a)] = tpv
                                B.wait("dve", tpv, t3)
                                t1 = B.sig("dve", dve.reciprocal(out=recm[a][:], in_=ps[6][:, :]))
                                bank_free[6] = t1
                                B.wait("dve", t1)
                                t2 = B.sig("dve", dve.tensor_tensor(out=tmpm[a][:], in0=ps[5][:, :], in1=recm[a][:], op=ALU.mult))
                                bank_free[5] = t2
                                B.wait("dve", t2, tzm, ym_tok[a])
                                t3 = B.sig("dve", dve.tensor_tensor(out=yom[a][:], in0=tmpm[a][:], in1=szm[a][:], op=ALU.mult))
                                bank_free[("szm", a)] = t3
                                bank_free[("recm", a)] = t3
                                B.wait("sp", t3)
                                ym_tok[a] = B.dma("sp", ygm_d[hh * 128:(hh + 1) * 128, c0:c0 + 512], yom[a][:], ymds[a])
                        if sweep < 3:
                            dl = DILS[sweep]
                            nbk = S // (128 * dl)
                            rot_done = [B.last("pool"), B.last("act"), B.last("dve")]
                            B.wait("pe", *rot_done)
                            blocks = [(r, n) for r in range(dl) for n in range(nbk)]
                            NBK_ = len(blocks)
                            NU, NV, SK = 4, 6, 2
                            s2_free = [None] * NU
                            p2_free = [None] * NU
                            nd_free = [None] * NU
                            pv_tok = [None] * NBK_
                            tvc_tok = [None] * NBK_
                            exp_tok2 = [None] * NBK_

                            def cls_(r, nn):
                                st = r + 128 * nn * dl
                                return slice(st, st + 127 * dl + 1, dl)

                            def stage_a(i):
                                r, n = blocks[i]
                                u = i % NU
                                vs = i % NV
                                tpb = ps[4 + u][:, 256:512].bitcast(BF16)[:, 0:128]
                                s2 = ps[u][:, 0:256]
                                B.wait("pe", nd_free[u])
                                ttp = B.sig("pe", pe.transpose(out=tpb, in_=vT2[:, cls_(r, n)], identity=ident_bf[:]))
                                lastreader = i - NV + 1
                                B.wait("dve", ttp, pv_tok[lastreader] if lastreader >= 0 else None)
                                tvc_tok[i] = B.sig("dve", dve.tensor_copy(out=Vc[vs][:], in_=tpb))
                                W = 256 if n > 0 else 128
                                B.wait("pe", s2_free[u])
                                pe.matmul(s2[:, 0:128], lhsT=kT2[:, cls_(r, n)], rhs=qT2[:, cls_(r, n)], start=True, stop=False)
                                mm = pe.matmul(s2[:, 0:128], lhsT=ident_bf[:], rhs=mcur_bf[:], start=False, stop=True)
                                if n > 0:
                                    pe.matmul(s2[:, 128:256], lhsT=kT2[:, cls_(r, n - 1)], rhs=qT2[:, cls_(r, n)], start=True, stop=False)
                                    mm = pe.matmul(s2[:, 128:256], lhsT=ident_bf[:], rhs=mprev_bf[:], start=False, stop=True)
                                ts2 = B.sig("pe", mm)
                                B.wait("act", ts2, p2_free[u])
                                tex = B.sig("act", act.activation(out=P2[u][:, 0:W], in_=s2[:, 0:W], func=AF.Exp, scale=SC))
                                s2_free[u] = tex
                                exp_tok2[i] = tex

                            def stage_b(i):
                                r, n = blocks[i]
                                u = i % NU
                                vs = i % NV
                                vp = (i - 1) % NV
                                nd = ps[4 + u]
                                B.wait("pe", exp_tok2[i], tvc_tok[i])
                                pe.matmul(nd[:, 0:128], lhsT=Vc[vs][:], rhs=P2[u][:, 0:128], start=True, stop=(n == 0))
                                if n > 0:
                                    pe.matmul(nd[:, 0:128], lhsT=Vc[vp][:], rhs=P2[u][:, 128:256], start=False, stop=True)
                                mm = pe.matmul(nd[:, 128:256], lhsT=ones_bf[:], rhs=P2[u][:, 0:128], start=True, stop=(n == 0))
                                if n > 0:
                                    mm = pe.matmul(nd[:, 128:256], lhsT=ones_bf[:], rhs=P2[u][:, 128:256], start=False, stop=True)
                                tnd = B.sig("pe", mm)
                                p2_free[u] = tnd
                                pv_tok[i] = tnd
                                B.wait("dve", tnd)
                                ndv = nd[:, 0:256].rearrange("p (a c) -> p a c", a=2)
                                if sweep == 0:
                                    tac = B.sig("dve", dve.tensor_copy(out=NDA[:, :, cls_(r, n)], in_=ndv))
                                else:
                                    tac = B.sig("dve", dve.tensor_tensor(out=NDA[:, :, cls_(r, n)], in0=ndv, in1=NDA[:, :, cls_(r, n)], op=ALU.add))
                                nd_free[u] = tac

                            for it in range(NBK_ + SK):
                                if it < NBK_:
                                    stage_a(it)
                                if it >= SK:
                                    stage_b(it - SK)
                        sweep_done = [B.last("pe"), B.last("act"), B.last("dve"), B.last("pool")]
                B.barrier()

        with ExitStack() as p3:
            Wgl = sb(p3, "Wgl", [128, KC, 3072], BF16)
            Wbr = [sb(p3, "Wbr%d" % i, [128, 4, D], BF16) for i in range(3)]
            Wo = sb(p3, "Wo", [128, KC, D], BF16)
            hs = [sb(p3, "hs3_%d" % i, [128, KC, 512], BF16) for i in range(2)]
            yg = [sb(p3, "yg%d" % b_, [128, 4, 512], BF16) for b_ in range(3)]
            G = sb(p3, "G", [128, 24, 512], BF16)
            mg = sb(p3, "mg", [128, KC, 512], BF16)
            m1 = sb(p3, "m1", [128, 512], F32)
            m2 = sb(p3, "m2", [128, 512], F32)
            m3 = sb(p3, "m3", [128, 512], F32)
            xtok = [sb(p3, "xtok%d" % i, [128, D], F32) for i in range(2)]
            ytok = [sb(p3, "ytok%d" % i, [128, D], F32) for i in range(2)]
            junk = sb(p3, "junk", [128, 512], BF16)
            ssq = [sb(p3, "ssq%d" % i, [128, 4], F32) for i in range(2)]
            lds = [B.dsem() for _ in range(2)]
            ygds = B.dsem()
            xds = [B.dsem() for _ in range(2)]
            ods = [B.dsem() for _ in range(2)]
            wds = B.dsem()
            B.wait("sp", wtok("wG"), wtok("wa"), wtok("wb"), wtok("wm"), wtok("wout"))
            for k in range(KC):
                B.dma("sp", Wgl[:, k, :], wbf_d[k * 128:(k + 1) * 128, GL:GL + 3072], wds)
            for b_, wd in enumerate((wabf_d, wbbf_d, wmbf_d)):
                B.dma("sp", Wbr[b_][:], wd.rearrange("(e p) c -> p e c", p=128), wds)
            tw = B.dma("sp", Wo[:], woutbf_d.rearrange("(k p) c -> p k c", p=128), wds)
            B.wait("pe", tw)
            ygd = (yga_d, ygb_d, ygm_d)
            hs_free = [None, None]
            yg_free = None
            G_free = None
            mg_free = None
            bank_free = {}
            xt_free = [None, None]
            yt_free = [None, None]
            tcnt = 0
            ltok = {}

            def load_h3(blk):
                s_ = blk % 2
                B.wait("sp", hs_free[s_])
                ltok[blk] = B.dma("sp", hs[s_][:], hT_v[:, :, blk * 512:(blk + 1) * 512], lds[s_])
            load_h3(0)
            xtk = {}

            def load_xt(ti):
                a_ = ti % 2
                B.wait("sp", xt_free[a_])
                xtk[ti] = B.dma("sp", xtok[a_][:], x_d[ti * 128:(ti + 1) * 128, :], xds[a_])
            load_xt(0)
            st3 = {"yg_free": None, "G_free": None, "mg_free": None, "tyg": None, "tmg": None}

            def load_yg(blk):
                c0 = blk * 512
                B.wait("sp", st3["yg_free"])
                for b_ in range(3):
                    st3["tyg"] = B.dma("sp", yg[b_][:], ygd[b_].rearrange("(e p) s -> p e s", p=128)[:, :, c0:c0 + 512], ygds)

            def gates(blk, lo, hi):
                s = blk % 2
                if lo == 0:
                    B.wait("pe", ltok[blk])
                    B.wait("act", st3["G_free"])
                for gi in range(lo, hi):
                    bk = gi % 2
                    B.wait("pe", bank_free.get(bk))
                    for k in range(KC):
                        mm = pe.matmul(ps[bk][:, :], lhsT=Wgl[:, k, gi * 128:(gi + 1) * 128], rhs=hs[s][:, k, :], start=(k == 0), stop=(k == KC - 1))
                    tp = B.sig("pe", mm)
                    B.wait("act", tp)
                    bank_free[bk] = B.sig("act", act.activation(out=G[:, gi, :], in_=ps[bk][:, :], func=AF.Sigmoid, bias=bmerge[:, gi:gi + 1]))
                if hi == 24:
                    st3["tg"] = B.last("act")
                    hs_free[s] = B.last("pe")

            def branch(blk):
                B.wait("dve", st3["tg"])
                B.wait("pool", st3["mg_free"])
                B.wait("pe", st3["tyg"])
                for c in range(KC):
                    tb = []
                    for b_ in range(3):
                        bk = 2 + b_
                        B.wait("pe", bank_free.get(bk))
                        for e_ in range(4):
                            mm = pe.matmul(ps[bk][:, :], lhsT=Wbr[b_][:, e_, c * 128:(c + 1) * 128], rhs=yg[b_][:, e_, :], start=(e_ == 0), stop=(e_ == 3))
                        tb.append(B.sig("pe", mm))
                    mt = (m1, m2, m3)
                    tms_ = []
                    for b_ in range(3):
                        B.wait("dve", tb[b_], bank_free.get("m"))
                        t_ = B.sig("dve", dve.tensor_tensor(out=mt[b_][:], in0=ps[2 + b_][:, :], in1=G[:, b_ * 8 + c, :], op=ALU.mult))
                        bank_free[2 + b_] = t_
                        tms_.append(t_)
                    B.wait("pool", *tms_)
                    ta1 = B.sig("pool", pool.tensor_tensor(out=m1[:], in0=m1[:], in1=m2[:], op=ALU.add))
                    B.wait("pool", ta1)
                    ta2 = B.sig("pool", pool.tensor_tensor(out=mg[:, c, :], in0=m1[:], in1=m3[:], op=ALU.add))
                    bank_free["m"] = ta2
                st3["G_free"] = B.last("dve")
                st3["yg_free"] = B.last("pe")
                st3["tmg"] = B.last("pool")

            def outproj(blk, tt):
                ti = blk * 4 + tt
                a = ti % 2
                r0 = ti * 128
                if ti + 1 < NT:
                    load_xt(ti + 1)
                tx = xtk[ti]
                B.wait("pe", st3["tmg"])
                for half in range(2):
                    bk = 5 + half
                    B.wait("pe", bank_free.get(bk))
                    for k in range(KC):
                        mm = pe.matmul(ps[bk][:, :], lhsT=mg[:, k, tt * 128:(tt + 1) * 128], rhs=Wo[:, k, half * 512:(half + 1) * 512], start=(k == 0), stop=(k == KC - 1))
                to = B.sig("pe", mm)
                if tt == 3:
                    st3["mg_free"] = to
                B.wait("act", to, bank_free.get(("ssq", a)))
                for half in range(2):
                    tsq = B.sig("act", act.activation(out=junk[:], in_=ps[5 + half][:, :], func=AF.Square, accum_out=ssq[a][:, half:half + 1]))
                B.wait("dve", tsq)
                t1 = B.sig("dve", dve.tensor_tensor(out=ssq[a][:, 2:3], in0=ssq[a][:, 0:1], in1=ssq[a][:, 1:2], op=ALU.add))
                B.wait("dve", t1)
                t2 = B.sig("dve", dve.tensor_scalar(out=ssq[a][:, 2:3], in0=ssq[a][:, 2:3], scalar1=1.0 / D, scalar2=EPS, op0=ALU.mult, op1=ALU.add))
                B.wait("act", t2)
                t3 = B.sig("act", act.activation(out=ssq[a][:, 3:4], in_=ssq[a][:, 2:3], func=AF.Sqrt))
                B.wait("dve", t3)
                t4 = B.sig("dve", dve.reciprocal(out=ssq[a][:, 3:4], in_=ssq[a][:, 3:4]))
                B.wait("dve", t4, yt_free[a])
                for half in range(2):
                    t5 = B.sig("dve", dve.scalar_tensor_tensor(out=ytok[a][:, half * 512:(half + 1) * 512], in0=ps[5 + half][:, :], scalar=ssq[a][:, 3:4], in1=gpost[:, half * 512:(half + 1) * 512], op0=ALU.mult, op1=ALU.mult))
                    bank_free[5 + half] = t5
                bank_free[("ssq", a)] = t5
                B.wait("pool", t5, tx)
                t6 = B.sig("pool", pool.tensor_tensor(out=ytok[a][:], in0=ytok[a][:], in1=xtok[a][:], op=ALU.add))
                xt_free[a] = t6
                B.wait("sp", t6)
                yt_free[a] = B.dma("sp", y_d[r0:r0 + 128, :], ytok[a][:], ods[a])

            load_yg(0)
            if NB > 1:
                load_h3(1)
            gates(0, 0, 24)
            branch(0)
            for blk in range(NB):
                nxt = blk + 1 < NB
                if nxt:
                    load_yg(blk + 1)
                    if blk + 2 < NB:
                        pass
                for tt in range(4):
                    if nxt:
                        if tt == 0 and blk + 2 < NB:
                            pass
                        gates(blk + 1, 6 * tt, 6 * tt + 6)
                        if tt == 3 and blk + 2 < NB:
                            load_h3(blk + 2)
                    outproj(blk, tt)
                if nxt:
                    branch(blk + 1)
            B.barrier()
    return nc


def _consts():
    ident = np.eye(128, dtype=np.float32)
    p = np.arange(128)[:, None]
    f = np.arange(128)[None, :]
    mcur = np.where(p <= f, 0.0, NEGM).astype(np.float32)
    mprev = np.where(p >= f, 0.0, NEGM).astype(np.float32)
    perm = np.zeros((32, 32), np.float32)
    for m in range(16):
        perm[m + 16, m] = -1.0
        perm[m, m + 16] = 1.0
    half = 16
    inv = (np.float32(500000.0) ** (-np.arange(half, dtype=np.float32) / np.float32(half))).astype(np.float32)
    cinv = np.zeros((64, 2), np.float32)
    for q in range(64):
        cinv[q, 0] = inv[q % 16]
        cinv[q, 1] = 0.0 if q < 32 else np.float32(np.pi / 2)
    return ident, mcur, mprev, perm, cinv


def make_in_map(b, S, x, mem, positions, norm_pre_g, norm_post_g, norm_mem_g, w_in, b_forget, b_merge,
                w_mem_kv, w_branch_a, w_branch_b, w_branch_m, w_out):
    ident, mcur, mprev, perm, cinv = _consts()
    f32 = np.float32
    return {
        "xT": np.ascontiguousarray(x[b].T, dtype=f32),
        "x": np.ascontiguousarray(x[b], dtype=f32),
        "memT": np.ascontiguousarray(mem[b].T, dtype=f32),
        "posr": np.ascontiguousarray(np.broadcast_to(positions[b][None, :], (64, S)), dtype=np.int32),
        "w_in": np.ascontiguousarray(w_in[0], dtype=f32),
        "w_mem": np.ascontiguousarray(w_mem_kv[0], dtype=f32),
        "w_a": np.ascontiguousarray(w_branch_a[0], dtype=f32),
        "w_b": np.ascontiguousarray(w_branch_b[0], dtype=f32),
        "w_m": np.ascontiguousarray(w_branch_m[0], dtype=f32),
        "w_out": np.ascontiguousarray(w_out[0], dtype=f32),
        "gpre": np.ascontiguousarray(norm_pre_g[0].reshape(KC, 128).T, dtype=f32),
        "gmem": np.ascontiguousarray(norm_mem_g[0].reshape(KC, 128).T, dtype=f32),
        "gpost": np.ascontiguousarray(np.broadcast_to(norm_post_g[0][None, :], (128, D)), dtype=f32),
        "bmerge": np.ascontiguousarray(b_merge[0].reshape(24, 128).T, dtype=f32),
        "bfg": np.ascontiguousarray(b_forget[0].reshape(8, 1), dtype=f32),
        "c_ident": ident, "c_mcur": mcur, "c_mprev": mprev, "c_perm": perm, "c_inv": cinv,
    }


def kernel(**inputs):
    inputs = {k: np.asarray(v) for k, v in inputs.items()}
    x = inputs["x"]
    Bsz, S, _ = x.shape
    nc = build_program(S)
    in_maps = [make_in_map(b, S, **inputs) for b in range(Bsz)]
    res = run_bass_kernel_spmd(nc, in_maps, core_ids=list(range(Bsz)))
    out = np.stack([np.asarray(r["y"], dtype=np.float32) for r in res.results], axis=0)
    return out
```
